# Optimizing a Trainium2 kernel written in Bass

```python
import jax
import jax.numpy as jnp
from jax import lax
import numpy as np

D_MODEL = 1024
BATCH = 4
SEQ = 8192
DEPTH = 4
DEC_BATCH = 32
DEC_SEQ = 32
PAST_LEN = 2048

CHUNK = 64
N_EVEN = (DEPTH + 1) // 2
N_ODD = DEPTH // 2
EPS = 1e-6

D_A = D_MODEL // 2
CONV_W = 31
CONV_PAD = CONV_W - 1

H_B = 4
DK_B = D_MODEL // 16
DV_B = D_MODEL // 8
D_BK = H_B * DK_B
D_BV = H_B * DV_B
GATE_RANK = 16
GATE_TAU = 16.0
GLA_CHUNK = CHUNK

D_C = D_MODEL
POOL_WINDOWS = (2, 4, 8, 16)
N_POOL_GROUPS = 4
POOL_GW = D_C // N_POOL_GROUPS
POOL_PAD = 15

EV_SPLITS = (D_A, 2 * D_A, 3 * D_A, 3 * D_A + D_BK, 3 * D_A + 2 * D_BK,
             3 * D_A + 2 * D_BK + D_BV, 3 * D_A + 2 * D_BK + 2 * D_BV)
EV_IN = 3 * D_A + 2 * D_BK + 2 * D_BV + GATE_RANK

kernel_name = "hybrid_conv_gla_pool_streaming_step"


def rms_norm(x, g):
    xf = x.astype(jnp.float32)
    y = xf * lax.rsqrt(jnp.mean(xf * xf, axis=-1, keepdims=True) + EPS)
    return (y * g.astype(jnp.float32)).astype(x.dtype)


def layer_norm(x, g, b):
    xf = x.astype(jnp.float32)
    xc = xf - jnp.mean(xf, axis=-1, keepdims=True)
    y = xc * lax.rsqrt(jnp.mean(xc * xc, axis=-1, keepdims=True) + EPS)
    return (y * g.astype(jnp.float32) + b.astype(jnp.float32)).astype(x.dtype)


def causal_depthwise_conv(glu, buf, w, b):
    xp = jnp.concatenate([buf.astype(glu.dtype), glu], axis=1)
    y = lax.conv_general_dilated(
        xp, w.astype(glu.dtype)[:, None, :], window_strides=(1,), padding="VALID",
        dimension_numbers=("NWC", "WIO", "NWC"), feature_group_count=D_A)
    return y + b.astype(glu.dtype), xp[:, -CONV_PAD:]


def gla_blocked(q, k, v, log_a, s0):
    B, L = q.shape[0], q.shape[1]
    C = min(GLA_CHUNK, L)
    N = L // C
    f32 = jnp.float32
    qc = q.astype(f32).reshape(B, N, C, H_B, DK_B) * (DK_B ** -0.5)
    kc = k.astype(f32).reshape(B, N, C, H_B, DK_B)
    vc = v.astype(f32).reshape(B, N, C, H_B, DV_B)
    b = jnp.cumsum(log_a.astype(f32).reshape(B, N, C, H_B, DK_B), axis=2)
    b_last = b[:, :, -1:]
    q_e = qc * jnp.exp(b)
    k_e = kc * jnp.exp(-b)
    k_l = kc * jnp.exp(b_last - b)
    mask = jnp.tril(jnp.ones((C, C), dtype=bool))
    att = jnp.where(mask, jnp.einsum("bnthd,bnshd->bnhts", q_e, k_e), 0.0)
    o = jnp.einsum("bnhts,bnshv->bnthv", att, vc)
    upd = jnp.einsum("bnshd,bnshv->bnhdv", k_l, vc)
    decay = jnp.exp(b_last[:, :, 0])

    def step(s, inp):
        d, u = inp
        return d[..., None] * s + u, s

    s_fin, s_start = lax.scan(step, s0.astype(f32),
                              (jnp.moveaxis(decay, 1, 0), jnp.moveaxis(upd, 1, 0)))
    s_start = jnp.moveaxis(s_start, 0, 1)
    o = o + jnp.einsum("bnthd,bnhdv->bnthv", q_e, s_start)
    return o.reshape(B, L, H_B, DV_B), s_fin.astype(s0.dtype)


def even_mixer(h, conv_buf, gla_s, w_in, conv_w, conv_b, ln_g, ln_b, gate_w2, gate_b, head_g, w_out):
    B, L, _ = h.shape
    a_val, a_glu, a_gate, q, k, v, b_gate, lr = jnp.split(h @ w_in, EV_SPLITS, axis=-1)
    glu = a_val * jax.nn.sigmoid(a_glu)
    ya, new_buf = causal_depthwise_conv(glu, conv_buf, conv_w, conv_b)
    ya = jax.nn.silu(layer_norm(ya, ln_g, ln_b)) * jax.nn.silu(a_gate)
    log_a = jax.nn.log_sigmoid((lr @ gate_w2 + gate_b).astype(jnp.float32)) / GATE_TAU
    o, new_s = gla_blocked(q.reshape(B, L, H_B, DK_B), k.reshape(B, L, H_B, DK_B),
                           v.reshape(B, L, H_B, DV_B), log_a.reshape(B, L, H_B, DK_B), gla_s)
    o = rms_norm(o, head_g.reshape(H_B, DV_B)).reshape(B, L, D_BV).astype(h.dtype)
    yb = o * jax.nn.silu(b_gate)
    y = jnp.concatenate([ya, yb], axis=-1) @ w_out
    return y, new_buf, new_s


def pool_mixer(h, pool_buf, pos0, w_in, group_w, group_b, scale, w_out):
    B, L, _ = h.shape
    v, gate = jnp.split(h @ w_in, 2, axis=-1)
    xp = jnp.concatenate([pool_buf.astype(v.dtype), v], axis=1)
    cs = jnp.cumsum(xp.astype(jnp.float32), axis=1)
    cs = jnp.pad(cs, ((0, 0), (1, 0), (0, 0)))
    pos = pos0 + jnp.arange(L)
    vf = v.astype(jnp.float32)
    outs = []
    for gi, w in enumerate(POOL_WINDOWS):
        sl = slice(gi * POOL_GW, (gi + 1) * POOL_GW)
        win_sum = cs[:, POOL_PAD + 1:POOL_PAD + 1 + L, sl] - cs[:, POOL_PAD + 1 - w:POOL_PAD + 1 - w + L, sl]
        cnt = jnp.minimum(pos + 1, w).astype(jnp.float32)
        outs.append(win_sum / cnt[None, :, None] - vf[..., sl])
    p = jnp.concatenate(outs, axis=-1).astype(h.dtype).reshape(B, L, N_POOL_GROUPS, POOL_GW)
    p = jnp.einsum("blgc,gcd->blgd", p, group_w).reshape(B, L, D_C) + group_b
    y = (p * scale * jax.nn.silu(gate)) @ w_out
    return y, xp[:, -POOL_PAD:]


def trunk(x, c, conv_st, gla_st, pool_st, pos0,
          norm_g, ada_w, ada_b, ev_w_in, ev_conv_w, ev_conv_b, ev_ln_g, ev_ln_b,
          ev_gate_w2, ev_gate_b, ev_head_g, ev_w_out,
          od_w_in, od_group_w, od_group_b, od_scale, od_w_out, final_g):
    cs = jax.nn.silu(c)
    conv_out, gla_out, pool_out = [], [], []
    for l in range(DEPTH):
        shift, scale, gate = jnp.split(cs @ ada_w[l] + ada_b[l], 3, axis=-1)
        h = rms_norm(x, norm_g[l]) * (1.0 + scale[:, None, :]) + shift[:, None, :]
        if l % 2 == 0:
            e = l // 2
            y, cb, gs = even_mixer(h, conv_st[e], gla_st[e], ev_w_in[e], ev_conv_w[e], ev_conv_b[e],
                                   ev_ln_g[e], ev_ln_b[e], ev_gate_w2[e], ev_gate_b[e],
                                   ev_head_g[e], ev_w_out[e])
            conv_out.append(cb)
            gla_out.append(gs)
        else:
            o = l // 2
            y, pb = pool_mixer(h, pool_st[o], pos0, od_w_in[o], od_group_w[o], od_group_b[o],
                               od_scale[o], od_w_out[o])
            pool_out.append(pb)
        x = x + gate[:, None, :] * y
    return rms_norm(x, final_g), jnp.stack(conv_out), jnp.stack(gla_out), jnp.stack(pool_out)


def setup_inputs(seed: int = 0) -> dict:
    key = jax.random.key(seed)
    ks = jax.random.split(key, 25)

    def nrm(k, shape, s):
        return jax.random.normal(k, shape, jnp.float32) * s

    return {
        "x_prompt": nrm(ks[0], (BATCH, SEQ, D_MODEL), 1.0),
        "x_sample": nrm(ks[1], (DEC_BATCH, DEC_SEQ, D_MODEL), 1.0),
        "c_prompt": nrm(ks[2], (BATCH, D_MODEL), 1.0),
        "c_sample": nrm(ks[3], (DEC_BATCH, D_MODEL), 1.0),
        "state_conv": nrm(ks[4], (N_EVEN, DEC_BATCH, CONV_PAD, D_A), 0.5),
        "state_gla": nrm(ks[5], (N_EVEN, DEC_BATCH, H_B, DK_B, DV_B), 1.0),
        "state_pool": nrm(ks[6], (N_ODD, DEC_BATCH, POOL_PAD, D_C), 1.0),
        "norm_g": 1.0 + nrm(ks[7], (DEPTH, D_MODEL), 0.05),
        "ada_w": nrm(ks[8], (DEPTH, D_MODEL, 3 * D_MODEL), 0.5 * D_MODEL ** -0.5),
        "ada_b": nrm(ks[9], (DEPTH, 3 * D_MODEL), 0.02),
        "ev_w_in": nrm(ks[10], (N_EVEN, D_MODEL, EV_IN), D_MODEL ** -0.5),
        "ev_conv_w": nrm(ks[11], (N_EVEN, CONV_W, D_A), CONV_W ** -0.5),
        "ev_conv_b": nrm(ks[12], (N_EVEN, D_A), 0.02),
        "ev_ln_g": 1.0 + nrm(ks[13], (N_EVEN, D_A), 0.05),
        "ev_ln_b": nrm(ks[14], (N_EVEN, D_A), 0.02),
        "ev_gate_w2": nrm(ks[15], (N_EVEN, GATE_RANK, D_BK), GATE_RANK ** -0.5),
        "ev_gate_b": nrm(ks[16], (N_EVEN, D_BK), 0.1),
        "ev_head_g": 1.0 + nrm(ks[17], (N_EVEN, D_BV), 0.05),
        "ev_w_out": nrm(ks[18], (N_EVEN, D_A + D_BV, D_MODEL), (D_A + D_BV) ** -0.5),
        "od_w_in": nrm(ks[19], (N_ODD, D_MODEL, 2 * D_C), D_MODEL ** -0.5),
        "od_group_w": nrm(ks[20], (N_ODD, N_POOL_GROUPS, POOL_GW, POOL_GW), POOL_GW ** -0.5),
        "od_group_b": nrm(ks[21], (N_ODD, D_C), 0.02),
        "od_scale": 1.0 + nrm(ks[22], (N_ODD, D_C), 0.1),
        "od_w_out": nrm(ks[23], (N_ODD, D_C, D_MODEL), D_C ** -0.5),
        "final_g": 1.0 + nrm(ks[24], (D_MODEL,), 0.05),
    }


def reference(x_prompt, x_sample, c_prompt, c_sample, state_conv, state_gla, state_pool,
              norm_g, ada_w, ada_b, ev_w_in, ev_conv_w, ev_conv_b, ev_ln_g, ev_ln_b,
              ev_gate_w2, ev_gate_b, ev_head_g, ev_w_out,
              od_w_in, od_group_w, od_group_b, od_scale, od_w_out, final_g):
    b_p = x_prompt.shape[0]
    zero_conv = jnp.zeros((N_EVEN, b_p, CONV_PAD, D_A), x_prompt.dtype)
    zero_gla = jnp.zeros((N_EVEN, b_p, H_B, DK_B, DV_B), state_gla.dtype)
    zero_pool = jnp.zeros((N_ODD, b_p, POOL_PAD, D_C), x_prompt.dtype)
    y_prompt, conv_p, gla_p, pool_p = trunk(
        x_prompt, c_prompt, zero_conv, zero_gla, zero_pool, 0,
        norm_g, ada_w, ada_b, ev_w_in, ev_conv_w, ev_conv_b, ev_ln_g, ev_ln_b,
        ev_gate_w2, ev_gate_b, ev_head_g, ev_w_out,
        od_w_in, od_group_w, od_group_b, od_scale, od_w_out, final_g)
    y_sample, conv_s, gla_s, pool_s = trunk(
        x_sample, c_sample, state_conv, state_gla, state_pool, PAST_LEN,
        norm_g, ada_w, ada_b, ev_w_in, ev_conv_w, ev_conv_b, ev_ln_g, ev_ln_b,
        ev_gate_w2, ev_gate_b, ev_head_g, ev_w_out,
        od_w_in, od_group_w, od_group_b, od_scale, od_w_out, final_g)
    return (y_prompt, y_sample, conv_p, gla_p, pool_p, conv_s, gla_s, pool_s)
```

```python
import os
import numpy as np
from contextlib import ExitStack
import concourse.bass as bass
import concourse.mybir as mybir
from concourse.bass_utils import run_bass_kernel_spmd

F32 = mybir.dt.float32
BF16 = mybir.dt.bfloat16
AF = mybir.ActivationFunctionType
ALU = mybir.AluOpType

D = 1024
TP = 256
TS = 128
LS = 32
EPS = 1e-6
GRAN = 128
ENGS = ("pe", "act", "dve", "pool", "sp")
DVE_TAPS = (1, 3, 5, 7, 8, 9, 11, 13, 15, 16, 17, 19, 21, 23, 24, 25, 27, 29)
SKIP_SAME_ENGINE_WAR = False

PC_NORMG = 0
PC_CONVB = 16
PC_LNG = 20
PC_LNB = 24
PC_HEADG = 28
PC_GRPB = 32
PC_SCALE = 40
PC_FING = 48
PC_CONVW = 56
PC_ADAB = 180
NPC = 228


class Sched:
    def __init__(self):
        self.ops = {e: [] for e in ENGS}
        self.cnt = {e: 0 for e in ENGS}
        self.known = {e: {} for e in ENGS}
        self.lastw = {}
        self.readers = {}
        self.dma_cnt = {}
        self.dma_last = {}
        self.const = set()
        self.semnames = set("e_" + e for e in ENGS)

    def res_of(self, a, is_read):
        if isinstance(a, (str, tuple)):
            return [a]
        name = a.tensor.name
        if is_read and name in self.const:
            return []
        if name.startswith("ps"):
            return [(name, 0)]
        es = mybir.dt.size(a.dtype)
        apl = list(a.ap)
        pstride = apl[0][0]
        lo = a.offset % pstride if pstride > 0 else a.offset
        hi = lo + sum((c - 1) * s for s, c in apl[1:]) + 1
        return [(name, g) for g in range(lo * es // GRAN, (hi * es - 1) // GRAN + 1)]

    def op(self, eng, fn, reads=(), writes=(), inc=True, dma=None):
        deps = []
        rres = [r for a in reads for r in self.res_of(a, True)]
        wres = [r for a in writes for r in self.res_of(a, False)]
        for r in rres:
            t = self.lastw.get(r)
            if t:
                deps.append(t)
        inorder = eng in ("act", "dve", "pool") and not dma and SKIP_SAME_ENGINE_WAR
        for r in wres:
            t = self.lastw.get(r)
            if t and not (inorder and t[2] == eng):
                deps.append(t)
            rd = self.readers.get(r)
            if rd:
                deps.extend((s, v, e) for s, (v, e) in rd.items() if not (inorder and e == eng))
        if dma and dma in self.dma_last:
            deps.append(self.dma_last[dma])
        kn = self.known[eng]
        need = {}
        for (sem, val, e) in deps:
            if e == eng and eng == "pe":
                continue
            if kn.get(sem, 0) >= val:
                continue
            if need.get(sem, 0) < val:
                need[sem] = val
        waits = []
        for sem, val in need.items():
            kn[sem] = val
            waits.append((sem, val))
        if dma:
            sn = "d_" + dma
            self.semnames.add(sn)
            self.dma_cnt[dma] = self.dma_cnt.get(dma, 0) + (1 if dma.startswith("cc") else 16)
            tok = (sn, self.dma_cnt[dma], "dma")
            incs = (sn, 1 if dma.startswith("cc") else 16)
            self.dma_last[dma] = tok
        elif inc:
            self.cnt[eng] += 1
            tok = ("e_" + eng, self.cnt[eng], eng)
            incs = ("e_" + eng, 1)
        else:
            tok = ("e_" + eng, self.cnt[eng] + 1, eng)
            incs = None
        self.ops[eng].append((waits, fn, incs))
        for r in rres:
            d = self.readers.setdefault(r, {})
            if d.get(tok[0], (0, None))[0] < tok[1]:
                d[tok[0]] = (tok[1], tok[2])
        for r in wres:
            self.lastw[r] = tok
            self.readers[r] = {}
        return tok

    def barrier(self):
        allv = {"e_" + e: self.cnt[e] for e in ENGS if self.cnt[e] > 0}
        for d, v in self.dma_cnt.items():
            allv["d_" + d] = v
        for e in ENGS:
            waits = []
            for sem, val in allv.items():
                if sem == "e_" + e and e == "pe":
                    continue
                if self.known[e].get(sem, 0) < val:
                    self.known[e][sem] = val
                    waits.append((sem, val))
            if waits:
                self.ops[e].append((waits, None, None))
        self.lastw = {}
        self.readers = {}

    def emit(self, nc, sems):
        with nc.Block() as block:
            def mk(en):
                def body(eng):
                    for waits, fn, incs in self.ops[en]:
                        for sem, val in waits:
                            eng.wait_ge(sems[sem], val)
                        if fn is None:
                            continue
                        ins = fn(eng)
                        if incs:
                            ins.then_inc(sems[incs[0]], incs[1])
                return body
            block.tensor(mk("pe"))
            block.scalar(mk("act"))
            block.vector(mk("dve"))
            block.gpsimd(mk("pool"))
            block.sync(mk("sp"))


def build_program(NPS, NSS, fused):
    R = 1 + 4 * NSS
    NTOK = NPS * TP + NSS * TS
    NSLOT = 2 + 4 * NSS
    NFL = 2 + NPS
    nc = bass.Bass("TRN2", target_bir_lowering=False)
    S = Sched()
    es = ExitStack()

    def din(name, shape, dt=F32):
        return nc.dram_tensor(name, list(shape), dt, kind="ExternalInput").ap()

    def dout(name, shape, dt=F32):
        return nc.dram_tensor(name, list(shape), dt, kind="ExternalOutput").ap()

    d_xT = din("xT", [D, NPS * TP])
    d_xsT = din("xsT", [D, NSS * TS])
    d_cT = din("cT", [128, 8 * R])
    d_convin = din("conv_in", [128, 4 * NSS * 4 * 30])
    d_glain = din("gla_in", [128, NSS * 4 * 2 * 128])
    d_poolin = din("pool_in", [128, 8 * NSS * 4 * 15])
    d_wei = din("w_ev_in", [D, 3088])
    d_weo = din("w_ev_out", [D, D])
    d_woi = din("w_od_in", [D, 2048])
    d_wog = din("w_od_grp", [128, 4 * 2 * 256])
    d_woo = din("w_od_out", [D, D])
    d_ada = din("ada_w", [2, D, 3 * D])
    d_pcols = din("pcols", [128, NPC])
    d_w2aug = din("w2aug", [32, 256])
    d_flags = din("flags", [128, NFL])
    d_consts = din("consts", [128, 648])
    d_consts2 = din("consts2", [128, 384])
    d_yT = dout("yT", [D, NTOK])
    d_convout = dout("conv_out", [128, 4 * NSLOT * 30])
    d_glaout = dout("gla_out", [128, NSLOT * 2 * 128])
    d_poolout = dout("pool_out", [128, 8 * NSLOT * 15])
    d_xoT = None if fused else dout("xoT", [D, NTOK])
    if fused:
        snd = [nc.dram_tensor(f"snd{i}", [D, TP], F32, kind="Internal").ap() for i in range(3)]
        rcv = [nc.dram_tensor(f"rcv{i}", [2 * D, TP], F32, kind="Internal").ap() for i in range(3)]
        sndS = [nc.dram_tensor(f"sndS{i}", [D, TS], F32, kind="Internal").ap() for i in range(3)]
        rcvS = [nc.dram_tensor(f"rcvS{i}", [2 * D, TS], F32, kind="Internal").ap() for i in range(3)]

    def sb(name, shape, dt):
        return es.enter_context(nc.sbuf_tensor("s_" + name, list(shape), dt))

    Wei = sb("Wei", [128, 8, 3088], BF16)
    Weo = sb("Weo", [128, 8, 1024], BF16)
    Woi = sb("Woi", [128, 8, 2048], BF16)
    Wog = sb("Wog", [128, 4, 2, 256], BF16)
    Woo = sb("Woo", [128, 8, 1024], BF16)
    pcols = sb("pcols", [128, NPC], F32)
    mod = sb("mod", [128, 2, 3, 8, R], F32)
    ones1024 = sb("ones1024", [128, 128], BF16)
    ones512 = sb("ones512", [128, 128], BF16)
    ones128 = sb("ones128", [128, 128], BF16)
    cst = sb("cst", [128, 648], F32)
    identb = sb("identb", [128, 128], BF16)
    maskb = sb("maskb", [128, 2, 2, 128], BF16)
    w2aug = sb("w2aug", [32, 256], F32)
    lrT = sb("lrT", [32, TP], F32)
    flags = sb("flags", [128, NFL], F32)
    SB32 = sb("SB32", [128, 5, 2, 128], F32)
    Sb16 = sb("Sb16", [128, 4, 2, 128], BF16)
    x = sb("x", [128, 8, TP], F32)
    x2 = sb("x2", [128, 8, TP], F32)
    xb = [x, x2]
    glu32 = sb("glu32", [128, 4, 286], F32)
    vp32 = sb("vp32", [128, 8, 271], F32)
    hb = sb("hb", [128, 8, TP], BF16)
    cat = sb("cat", [128, 8, TP], BF16)
    rstd = sb("rstd", [128, TP], F32)
    tmp32 = sb("tmp32", [128, 3, TP], F32)
    A32 = sb("A32", [128, 3072], F32)
    A16 = sb("A16", [128, 10496], BF16)
    PSB = [es.enter_context(nc.psum_tensor(f"ps{i}", [128, 512], F32)) for i in range(8)]

    UblkP = cst[:, 0:128]
    UblkS = cst[:, 128:256]
    SUP = cst[:, 256:384]
    SUS = cst[:, 384:512]
    cmaskP = cst[:, 512:514]
    cmaskS = cst[:, 516:520]
    delta = cst[:, 520:584]
    invw = cst[:, 584:648]
    tflat = tmp32[:, :, :].rearrange("p a b -> p (a b)")
    maskP32 = tflat[:, 256:384]
    maskS32 = tflat[:, 384:512]
    ident32 = tflat[:, 512:640]

    def carve(arena, off, shape):
        n = int(np.prod(shape))
        v = arena[:, off:off + n]
        if len(shape) == 1:
            return v
        if len(shape) == 2:
            return v.rearrange("p (a b) -> p a b", a=shape[0])
        if len(shape) == 3:
            return v.rearrange("p (a b c) -> p a b c", a=shape[0], b=shape[1])
        if len(shape) == 4:
            return v.rearrange("p (a b c d) -> p a b c d", a=shape[0], b=shape[1], c=shape[2])
        raise ValueError

    c32 = carve(A32, 0, [4, TP])
    loga = carve(A32, 1024, [2, 256])
    rstdo = carve(A32, 1024, [512])
    expb = carve(A32, 1536, [2, TP])
    expnb = carve(A32, 2048, [2, TP])
    meanb = carve(A32, 2048, [TP])
    varb = carve(A32, 2304, [TP])
    expd = carve(A32, 2560, [2, 256])
    win = [carve(A32, 0, [2, 271]), carve(A32, 544, [2, 271])]
    icb = carve(A32, 1088, [4, 16])
    rstd_fn = carve(A32, 1152, [TP])
    cb = carve(A16, 0, [4, TP])
    c2b = carve(A16, 1024, [4, TP])
    sga = carve(A16, 2048, [4, TP])
    ub = carve(A16, 3072, [2, TP])
    QE = carve(A16, 3584, [2, TP])
    KE = carve(A16, 4096, [2, TP])
    QZf = A16[:, 4608:5632]
    KLZf = A16[:, 5632:6656]
    v_tm = carve(A16, 6656, [2, 512])
    attm = carve(A16, 7680, [4, 128])
    osq = carve(A16, 8192, [512])
    sgb = carve(A16, 8704, [4, TP])
    prodr = carve(A16, 9728, [3, TP])
    sqn = carve(A16, 7680, [8, TP])
    pb = carve(A16, 0, [8, TP])
    sg = carve(A16, 2048, [8, TP])

    psi = [0]

    def PS():
        p = PSB[psi[0] % 6]
        psi[0] += 1
        return p

    def mm(out, lhsT, rhs, start, stop, reads, inc):
        S.op("pe", lambda e, o=out, l=lhsT, r=rhs, a=start, b=stop: e.matmul(o, l, r, start=a, stop=b),
             reads=reads, writes=[out], inc=inc)

    def mmgroup(out, pairs, reads):
        n = len(pairs)
        for i, (l, r) in enumerate(pairs):
            mm(out, l, r, i == 0, i == n - 1, reads, i == n - 1)

    def act(out, in_, func, bias=None, scale=None, extra_reads=()):
        kw = {}
        if bias is not None:
            kw["bias"] = bias
        if scale is not None:
            kw["scale"] = scale
        rd = [in_] + list(extra_reads)
        S.op("act", lambda e, o=out, i=in_, f=func, k=kw: e.activation(out=o, in_=i, func=f, **k),
             reads=rd, writes=[out])

    def tt(eng, out, in0, in1, op):
        S.op(eng, lambda e, o=out, a=in0, b=in1, p=op: e.tensor_tensor(o, a, b, p), reads=[in0, in1], writes=[out])

    def ts(eng, out, in0, s1, s2, op0, op1=None, extra_reads=()):
        if op1 is None:
            S.op(eng, lambda e, o=out, a=in0, x1=s1, p0=op0: e.tensor_scalar(o, a, x1, None, p0),
                 reads=[in0] + list(extra_reads), writes=[out])
        else:
            S.op(eng, lambda e, o=out, a=in0, x1=s1, x2=s2, p0=op0, p1=op1: e.tensor_scalar(o, a, x1, x2, p0, p1),
                 reads=[in0] + list(extra_reads), writes=[out])

    def stt(eng, out, in0, scalar, in1, op0, op1, extra_reads=()):
        S.op(eng, lambda e, o=out, a=in0, s=scalar, b=in1, p0=op0, p1=op1: e.scalar_tensor_tensor(o, a, s, b, p0, p1),
             reads=[in0, in1] + list(extra_reads), writes=[out])

    def cp(eng, out, in_):
        S.op(eng, lambda e, o=out, i=in_: e.tensor_copy(o, i), reads=[in_], writes=[out])

    def mset(eng, out, val):
        S.op(eng, lambda e, o=out, v=val: e.memset(o, v), writes=[out])

    def dma(q, out, in_, tag, reads=(), writes=()):
        S.op(q, lambda e, o=out, i=in_: e.dma_start(out=o, in_=i), reads=list(reads), writes=list(writes), dma=tag)

    def kp(ap2d, k):
        return ap2d.rearrange("(k p) n -> p k n", p=128)

    dma("pool", Wei[:], kp(d_wei, 8), "w0", writes=[Wei[:]])
    dma("pool", Weo[:], kp(d_weo, 8), "w1", writes=[Weo[:]])
    dma("pool", Woi[:], kp(d_woi, 8), "w2", writes=[Woi[:]])
    dma("pool", Wog[:], d_wog.rearrange("p (g k n) -> p g k n", g=4, k=2), "w3", writes=[Wog[:]])
    dma("sp", pcols[:], d_pcols, "c0", writes=[pcols[:]])
    dma("sp", cst[:], d_consts, "c1", writes=[cst[:]])
    dma("sp", tflat[:, 256:640], d_consts2, "c5", writes=[tflat[:, 256:640]])
    dma("sp", w2aug[:], d_w2aug, "c2", writes=[w2aug[:]])
    dma("sp", flags[:], d_flags, "c3", writes=[flags[:]])
    csT = tmp32[:, 0:1, 0:8 * R].rearrange("p a (k r) -> p (a k) r", k=8)
    dma("sp", tmp32[:, 0, 0:8 * R], d_cT, "c4", writes=[tmp32[:, 0, 0:8 * R]])
    act(tmp32[:, 0, 0:8 * R], tmp32[:, 0, 0:8 * R], AF.Silu)
    for hf in range(4):
        stg = A32[:, 0:2048].rearrange("p (k n) -> p k n", k=2)
        dma("sp", stg, kp(d_woo, 8)[:, hf * 2:(hf + 1) * 2, :], "w4", writes=[stg])
        for k4 in range(2):
            kc = hf * 2 + k4
            ts("dve", Woo[:, kc, :], stg[:, k4, :], pcols[:, PC_SCALE + kc:PC_SCALE + kc + 1], None, ALU.mult,
               extra_reads=[pcols[:]])
    mset("dve", ones1024[:], 1.0 / 1024)
    mset("dve", ones512[:], 1.0 / 512)
    mset("dve", ones128[:], 1.0 / 128)
    mset("pool", lrT[:], 1.0)
    mset("pool", SB32[:], 0.0)
    mset("pool", glu32[:], 0.0)
    mset("pool", vp32[:], 0.0)
    mset("pool", QZf, 0.0)
    for h in range(2):
        cp("dve", maskb[:, 0, h, :], maskP32)
        cp("dve", maskb[:, 1, h, :], maskS32)
    cp("dve", identb[:], ident32)
    NCH = 12
    stage = [x[:, :, :], vp32[:, :, 0:256]]
    pi = 0
    for li in range(2):
        for pc in range(NCH):
            st = stage[pi % 2]
            dma("sp", st, kp(d_ada[li], 8)[:, :, pc * 256:(pc + 1) * 256], f"a{pi % 2}", writes=[st])
            for sub in range(2):
                oc = pc * 2 + sub
                ps = PS()
                mmgroup(ps[:, 0:R], [(st[:, kc, sub * 128:(sub + 1) * 128], csT[:, kc, :]) for kc in range(8)],
                        reads=[st, tmp32[:, 0, 0:8 * R]])
                w, ko = oc // 8, oc % 8
                bcol = pcols[:, PC_ADAB + li * 24 + oc:PC_ADAB + li * 24 + oc + 1]
                if w == 1:
                    ts("dve", mod[:, li, 1, ko, :], ps[:, 0:R], bcol, 1.0, ALU.add, ALU.add, extra_reads=[pcols[:]])
                    ts("dve", mod[:, li, 1, ko, :], mod[:, li, 1, ko, :],
                       pcols[:, PC_NORMG + li * 8 + ko:PC_NORMG + li * 8 + ko + 1], None, ALU.mult, extra_reads=[pcols[:]])
                else:
                    ts("dve", mod[:, li, w, ko, :], ps[:, 0:R], bcol, None, ALU.add, extra_reads=[pcols[:]])
            pi += 1
    S.barrier()
    mset("pool", x[:], 0.0)
    mset("pool", x2[:], 0.0)
    mset("pool", vp32[:], 0.0)
    S.barrier()
    S.const = {t.tensor.name for t in (Wei[:], Weo[:], Woi[:], Wog[:], Woo[:], pcols[:], mod[:], ones1024[:], ones512[:],
                                       ones128[:], cst[:], maskb[:], w2aug[:], flags[:], identb[:])}

    def pc(i):
        return pcols[:, i:i + 1]

    def step_views(kind, kidx):
        if kind == "P":
            T, nseg, L = TP, 1, TP
            xsrc = kp(d_xT, 8)[:, :, kidx * TP:(kidx + 1) * TP]
        else:
            T, nseg, L = TS, 4, LS
            xsrc = kp(d_xsT, 8)[:, :, kidx * TS:(kidx + 1) * TS]
        G = glu32[:, :, 0:nseg * (30 + L)].rearrange("p c (s x) -> p c s x", s=nseg)
        V = vp32[:, :, 0:nseg * (15 + L)].rearrange("p c (s x) -> p c s x", s=nseg)
        return T, nseg, L, xsrc, G, V

    def step_rows(kind, kidx):
        return [0] if kind == "P" else [1 + 4 * kidx + i for i in range(4)]

    def stagings(T):
        h0 = hb[:, :, :].rearrange("p a b -> p (a b)").bitcast(F32)[:, 0:4 * T].rearrange("p (k t) -> p k t", k=4)
        h1 = A16[:, 5632:7680].bitcast(F32)[:, 0:4 * T].rearrange("p (k t) -> p k t", k=4)
        return h0, h1

    def prologue_early(kind, kidx, xbuf):
        T, nseg, L, xsrc, G, V = step_views(kind, kidx)
        dma("sp", xbuf[:, :, 0:T], xsrc, "xin", writes=[xbuf[:, :, 0:T]])
        if fused and kidx > 1:
            rbuf = (rcv if kind == "P" else rcvS)[(kidx - 2) % 3]
            rkey = ("rcv", kind, (kidx - 2) % 3)
            h0, h1 = stagings(T)
            rv = kp(rbuf[0:D, :], 8)
            dma("sp", h0, rv[:, 0:4, :], "rin0", reads=[rkey], writes=[h0])
            dma("sp", h1, rv[:, 4:8, :], "rin1", reads=[rkey], writes=[h1])
        if kind == "S":
            ci_ = d_convin.rearrange("p (c k s x) -> p c k s x", c=4, k=NSS, s=4)[:, :, kidx, :, :]
            for cc in range(4):
                dma("sp", G[:, cc, :, 0:30], ci_[:, cc, :, :], "sin0", writes=[G[:, cc, :, 0:30]])
            gi_ = d_glain.rearrange("p (k s j v) -> p k s j v", k=NSS, s=4, j=2)[:, kidx, :, :, :]
            dma("sp", SB32[:, 0:4, :, :], gi_, "sin1", writes=[SB32[:, 0:4, :, :]])

    def prologue_blend(kind, kidx, xbuf):
        T, nseg, L, xsrc, G, V = step_views(kind, kidx)
        if fused and kidx > 1:
            h0, h1 = stagings(T)
            stt("dve", xbuf[:, 0:4, 0:T], h0, flags[:, 0:1], xbuf[:, 0:4, 0:T], ALU.mult, ALU.add)
            stt("dve", xbuf[:, 4:8, 0:T], h1, flags[:, 0:1], xbuf[:, 4:8, 0:T], ALU.mult, ALU.add)

    def prologue_late(kind, kidx):
        T, nseg, L, xsrc, G, V = step_views(kind, kidx)
        if kind == "S":
            pi_ = d_poolin.rearrange("p (c k s x) -> p c k s x", c=8, k=NSS, s=4)[:, :, kidx, :, :]
            for cc in range(8):
                dma("sp", V[:, cc, :, 0:15], pi_[:, cc, :, :], "sin2", writes=[V[:, cc, :, 0:15]])
        if kind == "P" and fused and kidx == 2:
            ts("pool", SB32[:, 0, :, :], SB32[:, 0, :, :], flags[:, 1:2], None, ALU.mult)
            ts("pool", glu32[:, :, 0:30], glu32[:, :, 0:30], flags[:, 1:2], None, ALU.mult)
            ts("pool", vp32[:, :, 0:15], vp32[:, :, 0:15], flags[:, 1:2], None, ALU.mult)
        if kind == "S" and kidx == 0:
            mset("pool", QZf, 0.0)

    def modulate_gen(li, xbuf, T, nseg, L, rows, sq, rs, part="all"):
        if part in ("all", "stats"):
            mod_stats(xbuf, T, sq, rs)
        if part in ("all", "apply"):
            mod_apply(li, xbuf, T, nseg, L, rows, rs)

    def mod_stats(xbuf, T, sq, rs):
        for kc in range(8):
            act(sq[:, kc, 0:T], xbuf[:, kc, 0:T], AF.Square)
        ps = PS()
        mmgroup(ps[:, 0:T], [(ones1024[:], sq[:, kc, 0:T]) for kc in range(8)], reads=[sq[:, :, 0:T]])
        act(rs[:, 0:T], ps[:, 0:T], AF.Ln, bias=EPS, scale=1.0)
        act(rs[:, 0:T], rs[:, 0:T], AF.Exp, scale=-0.5)

    def mod_apply(li, xbuf, T, nseg, L, rows, rs):
        for kc in range(8):
            tb_ = tmp32[:, kc % 3, 0:T]
            for si in range(nseg):
                r = rows[si]
                sl = slice(si * L, (si + 1) * L)
                stt("dve", tb_[:, sl], xbuf[:, kc, sl], mod[:, li, 1, kc, r:r + 1], rs[:, sl], ALU.mult, ALU.mult)
                act(hb[:, kc, sl], tb_[:, sl], AF.Identity, bias=mod[:, li, 0, kc, r:r + 1], scale=1.0)

    class _Stop(Exception):
        pass
    _stage = int(os.environ.get("DBG_STAGE", "0"))

    def chk(n):
        if n == _stage:
            raise _Stop()

    def emit_step(kind, sidx, kidx, nxt, premod):
        xc = xb[sidx % 2]
        xn = xb[(sidx + 1) % 2]
        if kind == "P":
            T, nseg, L, C, nblk, cpb = TP, 1, TP, 64, 2, 2
            rows = [0]
            xsrc = kp(d_xT, 8)[:, :, kidx * TP:(kidx + 1) * TP]
            yoff = kidx * TP
            Ublk, SU, cmask, mk = UblkP, SUP, cmaskP, 0
        else:
            T, nseg, L, C, nblk, cpb = TS, 4, LS, 32, 1, 4
            rows = [1 + 4 * kidx + i for i in range(4)]
            xsrc = kp(d_xsT, 8)[:, :, kidx * TS:(kidx + 1) * TS]
            yoff = NPS * TP + kidx * TS
            Ublk, SU, cmask, mk = UblkS, SUS, cmaskS, 1
        G = glu32[:, :, 0:nseg * (30 + L)].rearrange("p c (s x) -> p c s x", s=nseg)
        V = vp32[:, :, 0:nseg * (15 + L)].rearrange("p c (s x) -> p c s x", s=nseg)
        QZ = QZf.rearrange("p (j t c n) -> p j t c n", j=2, t=nblk, c=cpb)
        KLZ = KLZf.rearrange("p (t c n) -> p t c n", t=nblk, c=cpb)

        def segv(ap):
            return ap.rearrange("p (s l) -> p s l", s=nseg)

        def modulate(li):
            modulate_gen(li, xc, T, nseg, L, rows, cat, rstd)

        sprinkle = [None]

        def proj_fm(W, col0, m=128):
            ps = PS()
            mmgroup(ps[0:m, 0:T], [(W[:, kc, col0:col0 + m], hb[:, kc, 0:T]) for kc in range(8)], reads=[hb[:, :, 0:T]])
            if sprinkle[0] is not None:
                sprinkle[0](3)
            return ps

        def residual(W, li, dst, ocs=range(8), ss=None):
            pend = None

            def flush(oc_prev):
                sqb, psb_ = ss
                mm(psb_[:, 0:T], ones1024[:], sqb[:, oc_prev, 0:T], oc_prev == 0, oc_prev == 7, [sqb[:, oc_prev, 0:T]], True)

            for oc in ocs:
                ps = PS()
                mmgroup(ps[:, 0:T], [(W[:, kc, oc * 128:(oc + 1) * 128], cat[:, kc, 0:T]) for kc in range(8)],
                        reads=[cat[:, :, 0:T]])
                if ss is not None and pend is not None:
                    flush(pend)
                for si in range(nseg):
                    r = rows[si]
                    sl = slice(si * L, (si + 1) * L)
                    stt("dve", dst[:, oc, sl], ps[:, sl], mod[:, li, 2, oc, r:r + 1], xc[:, oc, sl], ALU.mult, ALU.add)
                if ss is not None:
                    act(ss[0][:, oc, 0:T], dst[:, oc, 0:T], AF.Square)
                    pend = oc
            if ss is not None and pend is not None:
                flush(pend)

        chk(1)
        if not premod:
            modulate(0)
        chk(2)
        ps = proj_fm(Wei, 3072, m=16)
        act(lrT[0:16, 0:T], ps[0:16, 0:T], AF.Copy)
        tsls = [slice(tb * 128, (tb + 1) * 128) for tb in range(nblk)]
        pszs = []
        for tb in range(nblk):
            psz = PS()
            mm(psz[:, 0:256], lrT[0:17, tsls[tb]], w2aug[0:17, :], True, True, [lrT[0:17, tsls[tb]]], True)
            pszs.append(psz)
        for tb in range(nblk):
            act(loga[:, tb, :], pszs[tb][:, 0:256], AF.Exp, scale=-1.0)
        for tb in range(nblk):
            act(loga[:, tb, :], loga[:, tb, :], AF.Ln, bias=1.0, scale=1.0)
        for cc in range(4):
            psg = proj_fm(Wei, 512 + cc * 128)
            sig = tmp32[:, cc % 3, 0:T]
            act(sig, psg[:, 0:T], AF.Sigmoid)
            psv = proj_fm(Wei, cc * 128)
            tt("dve", G[:, cc, :, 30:30 + L], segv(psv[:, 0:T]), segv(sig), ALU.mult)
        conv_ops = []
        cpsum = [PSB[6], PSB[7]]
        pcount = [0]
        for ccp in range(2):
            for j in range(31):
                for cc in (2 * ccp, 2 * ccp + 1):
                    def one(cc=cc, j=j):
                        i_ = G[:, cc, :, j:j + L]
                        wc = pc(PC_CONVW + cc * 31 + j)
                        o_ = segv(c32[:, cc, 0:T])
                        if j not in DVE_TAPS:
                            slot = prodr[:, pcount[0] % 3, 0:T]
                            pcount[0] += 1
                            act(segv(slot), i_, AF.Copy, scale=wc)
                            mm(cpsum[cc % 2][:, 0:T], identb[:], slot, j == 0, j == 30, [slot], True)
                        elif j == 1:
                            ts("dve", o_, i_, wc, pc(PC_CONVB + cc), ALU.mult, ALU.add)
                        else:
                            stt("dve", o_, i_, wc, o_, ALU.mult, ALU.add)
                        if j == 30:
                            tt("dve", c32[:, cc, 0:T], c32[:, cc, 0:T], cpsum[cc % 2][:, 0:T], ALU.add)
                            act(cb[:, cc, 0:T], c32[:, cc, 0:T], AF.Copy)
                            act(c2b[:, cc, 0:T], c32[:, cc, 0:T], AF.Square)
                    conv_ops.append(one)
        conv_pos = [0]

        def pump(n):
            for _ in range(n):
                if conv_pos[0] < len(conv_ops):
                    conv_ops[conv_pos[0]]()
                    conv_pos[0] += 1

        chk(3)
        sprinkle[0] = pump
        pump(10)
        chk(4)
        pump(12)
        psbs, psds = [], []
        for tb in range(nblk):
            psb = PS()
            for j in range(2):
                mm(psb[:, j * 128:(j + 1) * 128], loga[:, tb, j * 128:(j + 1) * 128], Ublk, True, True,
                   [loga[:, tb, :]], True)
            psd = PS()
            mm(psd[:, 0:256], SU, loga[:, tb, :], True, True, [loga[:, tb, :]], True)
            psbs.append(psb)
            psds.append(psd)
        for tb in range(nblk):
            pb2 = psbs[tb][:, 0:256].rearrange("p (j n) -> p j n", j=2)
            act(expb[:, :, tsls[tb]], pb2, AF.Exp)
            act(expnb[:, :, tsls[tb]], pb2, AF.Exp, scale=-1.0)
            act(expd[:, tb, :], psds[tb][:, 0:256], AF.Exp)
        pump(12)
        chk(5)
        for j in range(2):
            ps = proj_fm(Wei, 1536 + j * 128)
            stt("dve", QE[:, j, 0:T], ps[:, 0:T], 0.125, expb[:, j, 0:T], ALU.mult, ALU.mult)
            for c in range(4):
                tb, ci = c // cpb, c % cpb
                cp("dve", QZ[:, j, tb, ci, ci * C:(ci + 1) * C], QE[:, j, c * C:(c + 1) * C])
            ps = proj_fm(Wei, 1792 + j * 128)
            tt("dve", KE[:, j, 0:T], ps[:, 0:T], expnb[:, j, 0:T], ALU.mult)
            pump(12)
        for tb in range(nblk):
            tsl = slice(tb * 128, (tb + 1) * 128)
            ps = PS()
            mmgroup(ps[:, 0:512], [(hb[:, kc, tsl], Wei[:, kc, 2048:2560]) for kc in range(8)], reads=[hb[:, :, tsl]])
            act(v_tm[:, tb, :], ps[:, 0:512], AF.Copy)
            ps = PS()
            mmgroup(ps[:, 0:256], [(hb[:, kc, tsl], Wei[:, kc, 1792:2048]) for kc in range(8)], reads=[hb[:, :, tsl]])
            for ci in range(cpb):
                stt("dve", KLZ[:, tb, ci, :], ps[:, 0:256], cmask[:, ci:ci + 1], expd[:, tb, :], ALU.mult, ALU.mult)
            pump(12)
        for cc in range(4):
            ps = proj_fm(Wei, 1024 + cc * 128)
            act(sga[:, cc, 0:T], ps[:, 0:T], AF.Silu)
        for cc in range(4):
            ps = proj_fm(Wei, 2560 + cc * 128)
            act(sgb[:, cc, 0:T], ps[:, 0:T], AF.Silu)
        chk(6)
        chk(7)
        cp("dve", Sb16[:, 0, :, :], SB32[:, 0, :, :]) if kind == "P" else cp("dve", Sb16[:, :, :, :], SB32[:, 0:4, :, :])
        for c in range(4):
            tb, ci = c // cpb, c % cpb
            psu = PS()
            for j in range(2):
                mm(psu[:, j * 256:(j + 1) * 256], KLZ[:, tb, ci, j * 128:(j + 1) * 128], v_tm[:, tb, j * 256:(j + 1) * 256],
                   True, True, [KLZ[:, tb, ci, :], v_tm[:, tb, :]], True)
            for j in range(2):
                for hh in range(2):
                    prt = slice(64 * hh, 64 * hh + 64)
                    dec = expb[prt, j, c * C + C - 1:c * C + C]
                    if kind == "P":
                        src, dst = SB32[prt, c, j, :], SB32[prt, c + 1, j, :]
                    else:
                        src, dst = SB32[prt, c, j, :], SB32[prt, c, j, :]
                    stt("dve", dst, src, dec, psu[prt, j * 256 + hh * 128:j * 256 + hh * 128 + 128], ALU.mult, ALU.add,
                        extra_reads=[dec])
            if kind == "P" and c < 3:
                cp("dve", Sb16[:, c + 1, :, :], SB32[:, c + 1, :, :])
            pump(8)
        chk(8)
        attm_l = [attm, carve(A16, 9728, [4, 128])]
        osq_l = [osq, carve(A16, 5632, [512])]
        rstdo_l = [rstdo, carve(A32, 2560, [512])]
        pso_l, psq_l = [], []
        for tb in range(nblk):
            tsl = slice(tb * 128, (tb + 1) * 128)
            am = attm_l[tb]
            psa2 = [PS(), PS()]
            for hh in range(2):
                prt = slice(64 * hh, 64 * hh + 64)
                for j in range(2):
                    mm(psa2[hh][:, j * 128:(j + 1) * 128], KE[prt, j, tsl], QE[prt, j, tsl], True, True,
                       [KE[prt, j, tsl], QE[prt, j, tsl]], True)
            attv = am.rearrange("p (j hh) n -> p j hh n", hh=2)
            for hh in range(2):
                tt("dve", attv[:, :, hh, :], psa2[hh][:, 0:256].rearrange("p (j n) -> p j n", j=2), maskb[:, mk, 0:2, :], ALU.mult)
            pso = PS()
            for h in range(4):
                j, hh = h // 2, h % 2
                prt = slice(64 * hh, 64 * hh + 64)
                oh = pso[:, h * 128:(h + 1) * 128]
                mm(oh, v_tm[:, tb, h * 128:(h + 1) * 128], am[:, h, :], True, False, [v_tm[:, tb, :], am[:, h, :]], False)
                for ci in range(cpb):
                    c = tb * cpb + ci
                    mm(oh, Sb16[prt, c, j, :], QZ[prt, j, tb, ci, :], False, ci == cpb - 1,
                       [Sb16[prt, c, j, :], QZ[prt, j, tb, ci, :]], ci == cpb - 1)
            act(osq_l[tb][:], pso[:, 0:512], AF.Square)
            psq = PS()
            mm(psq[:, 0:512], ones128[:], osq_l[tb][:], True, True, [osq_l[tb][:]], True)
            pso_l.append(pso)
            psq_l.append(psq)
            pump(10)
        pump(1000)
        psm, pse = PSB[6], PSB[7]
        mmgroup(psm[:, 0:T], [(ones512[:], cb[:, cc, 0:T]) for cc in range(4)], reads=[cb[:, :, 0:T]])
        mmgroup(pse[:, 0:T], [(ones512[:], c2b[:, cc, 0:T]) for cc in range(4)], reads=[c2b[:, :, 0:T]])
        act(meanb[:, 0:T], psm[:, 0:T], AF.Copy)
        act(varb[:, 0:T], psm[:, 0:T], AF.Square)
        tt("dve", varb[:, 0:T], pse[:, 0:T], varb[:, 0:T], ALU.subtract)
        for tb in range(nblk):
            act(rstdo_l[tb][:], psq_l[tb][:, 0:512], AF.Ln, bias=EPS, scale=1.0)
            act(rstdo_l[tb][:], rstdo_l[tb][:], AF.Exp, scale=-0.5)
        act(varb[:, 0:T], varb[:, 0:T], AF.Ln, bias=EPS, scale=1.0)
        act(varb[:, 0:T], varb[:, 0:T], AF.Exp, scale=-0.5)
        for tb in range(nblk):
            tsl = slice(tb * 128, (tb + 1) * 128)
            ro = rstdo_l[tb]
            tt("dve", ro[:], pso_l[tb][:, 0:512], ro[:], ALU.mult)
            for h in range(4):
                stt("dve", cat[:, 4 + h, tsl], ro[:, h * 128:(h + 1) * 128],
                    pc(PC_HEADG + h), sgb[:, h, tsl], ALU.mult, ALU.mult)
        for cc in range(4):
            t1 = tmp32[:, cc % 3, 0:T]
            tt("dve", t1, c32[:, cc, 0:T], meanb[:, 0:T], ALU.subtract)
            tt("dve", t1, t1, varb[:, 0:T], ALU.mult)
            act(ub[:, cc % 2, 0:T], t1, AF.Silu, bias=pc(PC_LNB + cc), scale=pc(PC_LNG + cc))
            tt("dve", cat[:, cc, 0:T], ub[:, cc % 2, 0:T], sga[:, cc, 0:T], ALU.mult)
        chk(9)
        pump(1000)
        chk(10)
        if kind == "P":
            cp("pool", SB32[:, 0, :, :], SB32[:, 4, :, :])
        slot = None
        if kind == "P":
            if kidx == NPS - 3:
                slot = 0
            elif kidx == NPS - 1:
                slot = 1
        co = d_convout.rearrange("p (c s x) -> p c s x", c=4, s=NSLOT)
        go = d_glaout.rearrange("p (s j v) -> p s j v", s=NSLOT, j=2)
        po = d_poolout.rearrange("p (c s x) -> p c s x", c=8, s=NSLOT)
        if kind == "P":
            if slot is not None:
                dma("sp", co[:, :, slot, :], G[:, :, 0, L:L + 30], "so0", reads=[G[:, :, 0, L:L + 30]])
                dma("sp", go[:, slot, :, :], SB32[:, 0, :, :], "so1", reads=[SB32[:, 0, :, :]])
            cp("pool", G[:, :, 0, 0:30], G[:, :, 0, L:L + 30])
        else:
            s0 = 2 + 4 * kidx
            for cc in range(4):
                dma("sp", co[:, cc, s0:s0 + 4, :], G[:, cc, :, L:L + 30], "so0", reads=[G[:, cc, :, L:L + 30]])
            dma("sp", go[:, s0:s0 + 4, :, :], SB32[:, 0:4, :, :], "so1", reads=[SB32[:, 0:4, :, :]])
        chk(11)
        residual(Weo, 0, xc, ss=(sqn, PSB[6]))
        chk(12)

        act(rstd[:, 0:T], PSB[6][:, 0:T], AF.Ln, bias=EPS, scale=1.0)
        act(rstd[:, 0:T], rstd[:, 0:T], AF.Exp, scale=-0.5)
        modulate_gen(1, xc, T, nseg, L, rows, None, rstd, part="apply")
        if kind == "P":
            stt("dve", icb[:, :].rearrange("p g n -> p (g n)"), delta, flags[:, 2 + kidx:3 + kidx], invw, ALU.mult, ALU.add)
        for oc in range(8):
            ps = proj_fm(Woi, oc * 128)
            act(V[:, oc, :, 15:15 + L], segv(ps[:, 0:T]), AF.Copy)
        for oc in range(8):
            ps = proj_fm(Woi, 1024 + oc * 128)
            act(sg[:, oc, 0:T], ps[:, 0:T], AF.Silu)
        if nxt is not None:
            prologue_early(*nxt, xn)
        XL = 15 + L
        for gi in range(4):
            w = float(1 << (gi + 1))
            for c2 in range(2):
                chn = 2 * gi + c2
                src = V[:, chn, :, :]
                lo = 0
                for lev in range(gi + 1):
                    sh = 1 << lev
                    dstb = win[lev % 2][:, c2, 0:nseg * XL].rearrange("p (s x) -> p s x", s=nseg)
                    nlo = lo + sh
                    tt("dve", dstb[:, :, nlo:XL], src[:, :, nlo:XL], src[:, :, nlo - sh:XL - sh], ALU.add)
                    src = dstb
                    lo = nlo
                pbv = pb[:, chn, 0:T].rearrange("p (s l) -> p s l", s=nseg)
                if kind == "P":
                    stt("dve", pbv[:, :, 16:L], src[:, :, 31:XL], 1.0 / w, V[:, chn, :, 31:XL], ALU.mult, ALU.subtract)
                    t16 = tmp32[:, c2, 0:16]
                    tt("dve", t16, src[:, 0, 15:31], icb[:, gi, :], ALU.mult)
                    tt("dve", pb[:, chn, 0:16], t16, V[:, chn, 0, 15:31], ALU.subtract)
                else:
                    stt("dve", pbv, src[:, :, 15:XL], 1.0 / w, V[:, chn, :, 15:XL], ALU.mult, ALU.subtract)
        if nxt is not None:
            prologue_blend(*nxt, xn)
            nk_, ni_ = nxt
            Tn, nsegn, Ln, _, _, _ = step_views(nk_, ni_)
            modulate_gen(0, xn, Tn, nsegn, Ln, step_rows(nk_, ni_), sqn, rstd, part="stats")
        for gi in range(4):
            for o2 in range(2):
                ps = PS()
                mmgroup(ps[:, 0:T], [(Wog[:, gi, k2, o2 * 128:(o2 + 1) * 128], pb[:, 2 * gi + k2, 0:T]) for k2 in range(2)],
                        reads=[pb[:, 2 * gi:2 * gi + 2, 0:T]])
                oc = 2 * gi + o2
                stt("dve", cat[:, oc, 0:T], ps[:, 0:T], pc(PC_GRPB + oc), sg[:, oc, 0:T], ALU.add, ALU.mult)
        if kind == "P":
            if slot is not None:
                dma("sp", po[:, :, slot, :], V[:, :, 0, L:L + 15], "so2", reads=[V[:, :, 0, L:L + 15]])
            cp("pool", V[:, :, 0, 0:15], V[:, :, 0, L:L + 15])
        else:
            s0 = 2 + 4 * kidx
            for cc in range(8):
                dma("sp", po[:, cc, s0:s0 + 4, :], V[:, cc, :, L:L + 15], "so2", reads=[V[:, cc, :, L:L + 15]])
        chk(13)
        if nxt is not None:
            modulate_gen(0, xn, Tn, nsegn, Ln, step_rows(nk_, ni_), sqn, rstd, part="apply")
        residual(Woo, 1, xc)
        chk(14)

        nk = NPS if kind == "P" else NSS
        if fused and kidx < nk - 2:
            par = kidx % 3
            sbuf_ = (snd if kind == "P" else sndS)[par]
            rbuf_ = (rcv if kind == "P" else rcvS)[par]
            skey = ("snd", kind, par)
            rkey2 = ("rcv", kind, par)
            dma("pool", kp(sbuf_, 8), xc[:, :, 0:T], "xs", reads=[xc[:, :, 0:T], skey], writes=[skey])
            S.op("pool", lambda e, a=sbuf_, b=rbuf_: e.collective_compute(
                "AllGather", ALU.bypass, replica_groups=[[0, 1], [2, 3], [4, 5], [6, 7]], ins=[a], outs=[b]),
                reads=[skey], writes=[rkey2], dma="cc")
        if not fused:
            dma("sp", kp(d_xoT, 8)[:, :, yoff:yoff + T], xc[:, :, 0:T], "xout", reads=[xc[:, :, 0:T]])
        sq = sg
        for kc in range(8):
            act(sq[:, kc, 0:T], xc[:, kc, 0:T], AF.Square)
        ps = PS()
        mmgroup(ps[:, 0:T], [(ones1024[:], sq[:, kc, 0:T]) for kc in range(8)], reads=[sq[:, :, 0:T]])
        act(rstd_fn[:, 0:T], ps[:, 0:T], AF.Ln, bias=EPS, scale=1.0)
        act(rstd_fn[:, 0:T], rstd_fn[:, 0:T], AF.Exp, scale=-0.5)
        for kc in range(8):
            stt("dve", xc[:, kc, 0:T], xc[:, kc, 0:T], pc(PC_FING + kc), rstd_fn[:, 0:T], ALU.mult, ALU.mult)
        dma("sp", kp(d_yT, 8)[:, :, yoff:yoff + T], xc[:, :, 0:T], "yout", reads=[xc[:, :, 0:T]])
        if nxt is not None:
            prologue_late(*nxt)

    _dbg = int(os.environ.get("DBG_STEPS", "999"))
    steps = [("P", k) for k in range(NPS)] + [("S", k) for k in range(NSS)]
    steps = steps[:_dbg]
    if steps:
        prologue_early(*steps[0], xb[0])
        prologue_blend(*steps[0], xb[0])
        prologue_late(*steps[0])
    for sidx, (kd, k) in enumerate(steps):
        nxt = steps[sidx + 1] if sidx + 1 < len(steps) else None
        try:
            emit_step(kd, sidx, k, nxt, sidx > 0)
        except _Stop:
            pass
    S.barrier()

    sems = {}
    for sn in sorted(S.semnames):
        sems[sn] = es.enter_context(nc.semaphore(sn))
    S.emit(nc, sems)
    es.close()
    return nc


def _fm(a2d):
    r, f = a2d.shape
    return np.ascontiguousarray(a2d.T.reshape(f // 128, 128, r).transpose(1, 0, 2))


def _consts():
    c = np.zeros((128, 1032), np.float32)
    s = np.arange(128)[:, None]
    t = np.arange(128)[None, :]
    for i, C in enumerate((64, 32)):
        same = (s // C) == (t // C)
        U = (same & (s <= t)).astype(np.float32)
        SU = (same & (s > t)).astype(np.float32)
        c[:, i * 128:(i + 1) * 128] = -U / 16.0
        c[:, 256 + i * 128:256 + (i + 1) * 128] = -SU / 16.0
        c[:, 512 + i * 128:512 + (i + 1) * 128] = U
    c[:, 904:1032] = np.eye(128, dtype=np.float32)
    p = np.arange(128)
    for ci in range(2):
        c[:, 768 + ci] = (p // 64 == ci)
    for ci in range(4):
        c[:, 772 + ci] = (p // 32 == ci)
    for gi in range(4):
        w = 2 ** (gi + 1)
        tt_ = np.arange(16)
        c[:, 776 + gi * 16:776 + (gi + 1) * 16] = (1.0 / np.minimum(tt_ + 1, w) - 1.0 / w)[None, :]
        c[:, 840 + gi * 16:840 + (gi + 1) * 16] = 1.0 / w
    c1 = np.ascontiguousarray(np.concatenate([c[:, 0:512], c[:, 768:904]], axis=1))
    c2 = np.ascontiguousarray(np.concatenate([c[:, 512:768], c[:, 904:1032]], axis=1))
    return c1, c2


def _pack_core(inp, role, q, NPS, NSS, fused, seqlen, x_override=None, xs_override=None):
    le = 0 if role == 0 else 2
    e = le // 2
    lo_ = le + 1
    o = lo_ // 2
    R = 1 + 4 * NSS
    f32 = np.float32
    m = {}
    shift = 2 if (fused and role == 1) else 0
    xT = np.zeros((D, NPS * TP), f32)
    if x_override is not None:
        xT[:, :x_override.shape[1]] = x_override
    elif not (fused and role == 1):
        xT[:, :seqlen] = inp["x_prompt"][q, :seqlen].T
    m["xT"] = xT
    xsT = np.zeros((D, NSS * TS), f32)
    cT = np.zeros((R, D), f32)
    cT[0] = inp["c_prompt"][q]
    conv_in = np.zeros((128, 4, NSS, 4, 30), f32)
    gla_in = np.zeros((128, NSS, 4, 2, 128), f32)
    pool_in = np.zeros((128, 8, NSS, 4, 15), f32)
    for k in range(2):
        ks = k + shift
        if ks >= NSS:
            continue
        for i in range(4):
            sq_ = 8 * q + 4 * k + i
            if xs_override is not None:
                xsT[:, ks * TS + i * LS: ks * TS + (i + 1) * LS] = xs_override[:, k * TS + i * LS:k * TS + (i + 1) * LS]
            elif not (fused and role == 1):
                xsT[:, ks * TS + i * LS: ks * TS + (i + 1) * LS] = inp["x_sample"][sq_].T
            cT[1 + 4 * ks + i] = inp["c_sample"][sq_]
            conv_in[:, :, ks, i, :] = inp["state_conv"][e, sq_].T.reshape(4, 128, 30).transpose(1, 0, 2)
            gla_in[:, ks, i, :, :] = inp["state_gla"][e, sq_].reshape(2, 128, 128).transpose(1, 0, 2)
            pool_in[:, :, ks, i, :] = inp["state_pool"][o, sq_].T.reshape(8, 128, 15).transpose(1, 0, 2)
    m["xsT"] = xsT
    m["cT"] = np.ascontiguousarray(cT.T.reshape(8, 128, R).transpose(1, 0, 2)).reshape(128, 8 * R)
    m["conv_in"] = conv_in.reshape(128, -1)
    m["gla_in"] = gla_in.reshape(128, -1)
    m["pool_in"] = pool_in.reshape(128, -1)
    m["w_ev_in"] = np.ascontiguousarray(inp["ev_w_in"][e])
    m["w_ev_out"] = np.ascontiguousarray(inp["ev_w_out"][e])
    m["w_od_in"] = np.ascontiguousarray(inp["od_w_in"][o])
    gw = inp["od_group_w"][o]
    m["w_od_grp"] = np.ascontiguousarray(gw.reshape(4, 2, 128, 256).transpose(2, 0, 1, 3)).reshape(128, -1)
    m["w_od_out"] = np.ascontiguousarray(inp["od_w_out"][o])
    m["ada_w"] = np.ascontiguousarray(np.stack([inp["ada_w"][le], inp["ada_w"][lo_]]))
    pcl = np.zeros((128, NPC), f32)

    def col(v):
        return v.reshape(-1, 128).T
    pcl[:, PC_NORMG:PC_NORMG + 8] = col(inp["norm_g"][le])
    pcl[:, PC_NORMG + 8:PC_NORMG + 16] = col(inp["norm_g"][lo_])
    pcl[:, PC_CONVB:PC_CONVB + 4] = col(inp["ev_conv_b"][e])
    pcl[:, PC_LNG:PC_LNG + 4] = col(inp["ev_ln_g"][e])
    pcl[:, PC_LNB:PC_LNB + 4] = col(inp["ev_ln_b"][e])
    pcl[:, PC_HEADG:PC_HEADG + 4] = col(inp["ev_head_g"][e])
    pcl[:, PC_GRPB:PC_GRPB + 8] = col(inp["od_group_b"][o])
    pcl[:, PC_SCALE:PC_SCALE + 8] = col(inp["od_scale"][o])
    pcl[:, PC_FING:PC_FING + 8] = col(inp["final_g"])
    cw = inp["ev_conv_w"][e]
    pcl[:, PC_CONVW:PC_CONVW + 124] = cw.T.reshape(4, 128, 31).transpose(1, 0, 2).reshape(128, 124)
    pcl[:, PC_ADAB:PC_ADAB + 24] = col(inp["ada_b"][le])
    pcl[:, PC_ADAB + 24:PC_ADAB + 48] = col(inp["ada_b"][lo_])
    m["pcols"] = pcl
    w2 = np.zeros((32, 256), f32)
    w2[0:16] = inp["ev_gate_w2"][e]
    w2[16] = inp["ev_gate_b"][e]
    m["w2aug"] = w2
    fl = np.zeros((128, 2 + NPS), f32)
    fl[:, 0] = 1.0 if role == 1 else 0.0
    fl[:, 1] = 0.0 if (fused and role == 1) else 1.0
    fl[:, 2 + shift] = 1.0
    m["flags"] = fl
    m["consts"], m["consts2"] = _consts()
    return m


def _unfm(a, rows):
    k = a.shape[1]
    return np.ascontiguousarray(a.transpose(2, 1, 0).reshape(rows, k * 128))


def run_all(inp, seqlen=8192, fused=True):
    NP = seqlen // TP
    outs = [np.zeros((4, seqlen, D), np.float32), np.zeros((32, 32, D), np.float32),
            np.zeros((2, 4, 30, 512), np.float32), np.zeros((2, 4, 4, 64, 128), np.float32),
            np.zeros((2, 4, 15, 1024), np.float32), np.zeros((2, 32, 30, 512), np.float32),
            np.zeros((2, 32, 4, 64, 128), np.float32), np.zeros((2, 32, 15, 1024), np.float32)]

    def collect(res, role, q, NPS, NSS, shift, final):
        NSLOT = 2 + 4 * NSS
        e = role
        co = res["conv_out"].reshape(128, 4, NSLOT, 30)
        go = res["gla_out"].reshape(128, NSLOT, 2, 128)
        po = res["pool_out"].reshape(128, 8, NSLOT, 15)
        pslot = 1 if (shift > 0 or not fused) else 0
        outs[2][e, q] = _unfm(co[:, :, pslot, :], 30)
        outs[3][e, q] = go[:, pslot].transpose(1, 0, 2).reshape(4, 64, 128)
        outs[4][e, q] = _unfm(po[:, :, pslot, :], 15)
        for k in range(2):
            ks = k + shift
            for i in range(4):
                sq_ = 8 * q + 4 * k + i
                sl = 2 + 4 * ks + i
                outs[5][e, sq_] = _unfm(co[:, :, sl, :], 30)
                outs[6][e, sq_] = go[:, sl].transpose(1, 0, 2).reshape(4, 64, 128)
                outs[7][e, sq_] = _unfm(po[:, :, sl, :], 15)
        if final:
            yT = res["yT"]
            outs[0][q] = yT[:, shift * TP: shift * TP + seqlen].T
            for k in range(2):
                ks = k + shift
                for i in range(4):
                    base = NPS * TP + ks * TS + i * LS
                    outs[1][8 * q + 4 * k + i] = yT[:, base:base + LS].T

    if fused:
        NPS, NSS = NP + 2, 4
        nc = build_program(NPS, NSS, True)
        maps = []
        for c in range(8):
            maps.append(_pack_core(inp, c % 2, c // 2, NPS, NSS, True, seqlen))
        res = run_bass_kernel_spmd(nc, maps, core_ids=list(range(8)))
        for c in range(8):
            collect(res.results[c], c % 2, c // 2, NPS, NSS, 2 * (c % 2), c % 2 == 1)
    else:
        NPS, NSS = NP, 2
        nc = build_program(NPS, NSS, False)
        maps = [_pack_core(inp, 0, q, NPS, NSS, False, seqlen) for q in range(4)]
        maps += [_pack_core(inp, 0, q, NPS, NSS, False, seqlen) for q in range(4)]
        res = run_bass_kernel_spmd(nc, maps, core_ids=list(range(8)))
        for q in range(4):
            collect(res.results[q], 0, q, NPS, NSS, 0, False)
        maps2 = []
        for q in range(4):
            yT = res.results[q]["xoT"]
            maps2.append(_pack_core(inp, 1, q, NPS, NSS, False, seqlen, x_override=yT[:, :seqlen],
                                    xs_override=yT[:, NPS * TP:]))
        maps2 += maps2
        nc2 = build_program(NPS, NSS, False)
        res2 = run_bass_kernel_spmd(nc2, maps2, core_ids=list(range(8)))
        for q in range(4):
            collect(res2.results[q], 1, q, NPS, NSS, 0, True)
    return tuple(outs)


def kernel(**inputs):
    inp = {k: np.asarray(v) for k, v in inputs.items()}
    return run_all(inp, seqlen=inp["x_prompt"].shape[1], fused=True)
```

```python
import os
import numpy as np
from contextlib import ExitStack
import concourse.bass as bass
import concourse.mybir as mybir
from concourse.bass_utils import run_bass_kernel_spmd

F32 = mybir.dt.float32
BF16 = mybir.dt.bfloat16
AF = mybir.ActivationFunctionType
ALU = mybir.AluOpType

D = 1024
TP = 256
TS = 128
LS = 32
EPS = 1e-6
GRAN = 128
ENGS = ("pe", "act", "dve", "pool", "sp")
DVE_TAPS = (1, 3, 5, 7, 8, 9, 11, 13, 15, 16, 17, 19, 21, 23, 24, 25, 27, 29)
SKIP_SAME_ENGINE_WAR = False

PC_NORMG = 0
PC_CONVB = 16
PC_LNG = 20
PC_LNB = 24
PC_HEADG = 28
PC_GRPB = 32
PC_SCALE = 40
PC_FING = 48
PC_CONVW = 56
PC_ADAB = 180
NPC = 228


class Sched:
    def __init__(self):
        self.ops = {e: [] for e in ENGS}
        self.cnt = {e: 0 for e in ENGS}
        self.known = {e: {} for e in ENGS}
        self.lastw = {}
        self.readers = {}
        self.dma_cnt = {}
        self.dma_last = {}
        self.const = set()
        self.semnames = set("e_" + e for e in ENGS)

    def res_of(self, a, is_read):
        if isinstance(a, (str, tuple)):
            return [a]
        name = a.tensor.name
        if is_read and name in self.const:
            return []
        if name.startswith("ps"):
            return [(name, 0)]
        es = mybir.dt.size(a.dtype)
        apl = list(a.ap)
        pstride = apl[0][0]
        lo = a.offset % pstride if pstride > 0 else a.offset
        hi = lo + sum((c - 1) * s for s, c in apl[1:]) + 1
        return [(name, g) for g in range(lo * es // GRAN, (hi * es - 1) // GRAN + 1)]

    def op(self, eng, fn, reads=(), writes=(), inc=True, dma=None):
        deps = []
        rres = [r for a in reads for r in self.res_of(a, True)]
        wres = [r for a in writes for r in self.res_of(a, False)]
        for r in rres:
            t = self.lastw.get(r)
            if t:
                deps.append(t)
        inorder = eng in ("act", "dve", "pool") and not dma and SKIP_SAME_ENGINE_WAR
        for r in wres:
            t = self.lastw.get(r)
            if t and not (inorder and t[2] == eng):
                deps.append(t)
            rd = self.readers.get(r)
            if rd:
                deps.extend((s, v, e) for s, (v, e) in rd.items() if not (inorder and e == eng))
        if dma and dma in self.dma_last:
            deps.append(self.dma_last[dma])
        kn = self.known[eng]
        need = {}
        for (sem, val, e) in deps:
            if e == eng and eng == "pe":
                continue
            if kn.get(sem, 0) >= val:
                continue
            if need.get(sem, 0) < val:
                need[sem] = val
        waits = []
        for sem, val in need.items():
            kn[sem] = val
            waits.append((sem, val))
        if dma:
            sn = "d_" + dma
            self.semnames.add(sn)
            self.dma_cnt[dma] = self.dma_cnt.get(dma, 0) + (1 if dma.startswith("cc") else 16)
            tok = (sn, self.dma_cnt[dma], "dma")
            incs = (sn, 1 if dma.startswith("cc") else 16)
            self.dma_last[dma] = tok
        elif inc:
            self.cnt[eng] += 1
            tok = ("e_" + eng, self.cnt[eng], eng)
            incs = ("e_" + eng, 1)
        else:
            tok = ("e_" + eng, self.cnt[eng] + 1, eng)
            incs = None
        self.ops[eng].append((waits, fn, incs))
        for r in rres:
            d = self.readers.setdefault(r, {})
            if d.get(tok[0], (0, None))[0] < tok[1]:
                d[tok[0]] = (tok[1], tok[2])
        for r in wres:
            self.lastw[r] = tok
            self.readers[r] = {}
        return tok

    def barrier(self):
        allv = {"e_" + e: self.cnt[e] for e in ENGS if self.cnt[e] > 0}
        for d, v in self.dma_cnt.items():
            allv["d_" + d] = v
        for e in ENGS:
            waits = []
            for sem, val in allv.items():
                if sem == "e_" + e and e == "pe":
                    continue
                if self.known[e].get(sem, 0) < val:
                    self.known[e][sem] = val
                    waits.append((sem, val))
            if waits:
                self.ops[e].append((waits, None, None))
        self.lastw = {}
        self.readers = {}

    def emit(self, nc, sems):
        with nc.Block() as block:
            def mk(en):
                def body(eng):
                    for waits, fn, incs in self.ops[en]:
                        for sem, val in waits:
                            eng.wait_ge(sems[sem], val)
                        if fn is None:
                            continue
                        ins = fn(eng)
                        if incs:
                            ins.then_inc(sems[incs[0]], incs[1])
                return body
            block.tensor(mk("pe"))
            block.scalar(mk("act"))
            block.vector(mk("dve"))
            block.gpsimd(mk("pool"))
            block.sync(mk("sp"))


def build_program(NPS, NSS, fused):
    R = 1 + 4 * NSS
    NTOK = NPS * TP + NSS * TS
    NSLOT = 2 + 4 * NSS
    NFL = 2 + NPS
    nc = bass.Bass("TRN2", target_bir_lowering=False)
    S = Sched()
    es = ExitStack()

    def din(name, shape, dt=F32):
        return nc.dram_tensor(name, list(shape), dt, kind="ExternalInput").ap()

    def dout(name, shape, dt=F32):
        return nc.dram_tensor(name, list(shape), dt, kind="ExternalOutput").ap()

    d_xT = din("xT", [D, NPS * TP])
    d_xsT = din("xsT", [D, NSS * TS])
    d_cT = din("cT", [128, 8 * R])
    d_convin = din("conv_in", [128, 4 * NSS * 4 * 30])
    d_glain = din("gla_in", [128, NSS * 4 * 2 * 128])
    d_poolin = din("pool_in", [128, 8 * NSS * 4 * 15])
    d_wei = din("w_ev_in", [D, 3088])
    d_weo = din("w_ev_out", [D, D])
    d_woi = din("w_od_in", [D, 2048])
    d_wog = din("w_od_grp", [128, 4 * 2 * 256])
    d_woo = din("w_od_out", [D, D])
    d_ada = din("ada_w", [2, D, 3 * D])
    d_pcols = din("pcols", [128, NPC])
    d_w2aug = din("w2aug", [32, 256])
    d_flags = din("flags", [128, NFL])
    d_consts = din("consts", [128, 648])
    d_consts2 = din("consts2", [128, 384])
    d_yT = dout("yT", [D, NTOK])
    d_convout = dout("conv_out", [128, 4 * NSLOT * 30])
    d_glaout = dout("gla_out", [128, NSLOT * 2 * 128])
    d_poolout = dout("pool_out", [128, 8 * NSLOT * 15])
    d_xoT = None if fused else dout("xoT", [D, NTOK])
    if fused:
        snd = [nc.dram_tensor(f"snd{i}", [D, TP], F32, kind="Internal").ap() for i in range(3)]
        rcv = [nc.dram_tensor(f"rcv{i}", [2 * D, TP], F32, kind="Internal").ap() for i in range(3)]
        sndS = [nc.dram_tensor(f"sndS{i}", [D, TS], F32, kind="Internal").ap() for i in range(3)]
        rcvS = [nc.dram_tensor(f"rcvS{i}", [2 * D, TS], F32, kind="Internal").ap() for i in range(3)]

    def sb(name, shape, dt):
        return es.enter_context(nc.sbuf_tensor("s_" + name, list(shape), dt))

    Wei = sb("Wei", [128, 8, 3088], BF16)
    Weo = sb("Weo", [128, 8, 1024], BF16)
    Woi = sb("Woi", [128, 8, 2048], BF16)
    Wog = sb("Wog", [128, 4, 2, 256], BF16)
    Woo = sb("Woo", [128, 8, 1024], BF16)
    pcols = sb("pcols", [128, NPC], F32)
    mod = sb("mod", [128, 2, 3, 8, R], F32)
    ones1024 = sb("ones1024", [128, 128], BF16)
    ones512 = sb("ones512", [128, 128], BF16)
    ones128 = sb("ones128", [128, 128], BF16)
    cst = sb("cst", [128, 648], F32)
    identb = sb("identb", [128, 128], BF16)
    maskb = sb("maskb", [128, 2, 2, 128], BF16)
    w2aug = sb("w2aug", [32, 256], F32)
    lrT = sb("lrT", [32, TP], F32)
    flags = sb("flags", [128, NFL], F32)
    SB32 = sb("SB32", [128, 5, 2, 128], F32)
    Sb16 = sb("Sb16", [128, 4, 2, 128], BF16)
    x = sb("x", [128, 8, TP], F32)
    x2 = sb("x2", [128, 8, TP], F32)
    xb = [x, x2]
    glu32 = sb("glu32", [128, 4, 286], F32)
    vp32 = sb("vp32", [128, 8, 271], F32)
    hb = sb("hb", [128, 8, TP], BF16)
    cat = sb("cat", [128, 8, TP], BF16)
    rstd = sb("rstd", [128, TP], F32)
    tmp32 = sb("tmp32", [128, 3, TP], F32)
    A32 = sb("A32", [128, 3072], F32)
    A16 = sb("A16", [128, 10496], BF16)
    PSB = [es.enter_context(nc.psum_tensor(f"ps{i}", [128, 512], F32)) for i in range(8)]

    UblkP = cst[:, 0:128]
    UblkS = cst[:, 128:256]
    SUP = cst[:, 256:384]
    SUS = cst[:, 384:512]
    cmaskP = cst[:, 512:514]
    cmaskS = cst[:, 516:520]
    delta = cst[:, 520:584]
    invw = cst[:, 584:648]
    tflat = tmp32[:, :, :].rearrange("p a b -> p (a b)")
    maskP32 = tflat[:, 256:384]
    maskS32 = tflat[:, 384:512]
    ident32 = tflat[:, 512:640]

    def carve(arena, off, shape):
        n = int(np.prod(shape))
        v = arena[:, off:off + n]
        if len(shape) == 1:
            return v
        if len(shape) == 2:
            return v.rearrange("p (a b) -> p a b", a=shape[0])
        if len(shape) == 3:
            return v.rearrange("p (a b c) -> p a b c", a=shape[0], b=shape[1])
        if len(shape) == 4:
            return v.rearrange("p (a b c d) -> p a b c d", a=shape[0], b=shape[1], c=shape[2])
        raise ValueError

    c32 = carve(A32, 0, [4, TP])
    loga = carve(A32, 1024, [2, 256])
    rstdo = carve(A32, 1024, [512])
    expb = carve(A32, 1536, [2, TP])
    expnb = carve(A32, 2048, [2, TP])
    meanb = carve(A32, 2048, [TP])
    varb = carve(A32, 2304, [TP])
    expd = carve(A32, 2560, [2, 256])
    win = [carve(A32, 0, [2, 271]), carve(A32, 544, [2, 271])]
    icb = carve(A32, 1088, [4, 16])
    rstd_fn = carve(A32, 1152, [TP])
    cb = carve(A16, 0, [4, TP])
    c2b = carve(A16, 1024, [4, TP])
    sga = carve(A16, 2048, [4, TP])
    ub = carve(A16, 3072, [2, TP])
    QE = carve(A16, 3584, [2, TP])
    KE = carve(A16, 4096, [2, TP])
    QZf = A16[:, 4608:5632]
    KLZf = A16[:, 5632:6656]
    v_tm = carve(A16, 6656, [2, 512])
    attm = carve(A16, 7680, [4, 128])
    osq = carve(A16, 8192, [512])
    sgb = carve(A16, 8704, [4, TP])
    prodr = carve(A16, 9728, [3, TP])
    sqn = carve(A16, 7680, [8, TP])
    pb = carve(A16, 0, [8, TP])
    sg = carve(A16, 2048, [8, TP])

    psi = [0]

    def PS():
        p = PSB[psi[0] % 6]
        psi[0] += 1
        return p

    def mm(out, lhsT, rhs, start, stop, reads, inc):
        S.op("pe", lambda e, o=out, l=lhsT, r=rhs, a=start, b=stop: e.matmul(o, l, r, start=a, stop=b),
             reads=reads, writes=[out], inc=inc)

    def mmgroup(out, pairs, reads):
        n = len(pairs)
        for i, (l, r) in enumerate(pairs):
            mm(out, l, r, i == 0, i == n - 1, reads, i == n - 1)

    def act(out, in_, func, bias=None, scale=None, extra_reads=()):
        kw = {}
        if bias is not None:
            kw["bias"] = bias
        if scale is not None:
            kw["scale"] = scale
        rd = [in_] + list(extra_reads)
        S.op("act", lambda e, o=out, i=in_, f=func, k=kw: e.activation(out=o, in_=i, func=f, **k),
             reads=rd, writes=[out])

    def tt(eng, out, in0, in1, op):
        S.op(eng, lambda e, o=out, a=in0, b=in1, p=op: e.tensor_tensor(o, a, b, p), reads=[in0, in1], writes=[out])

    def ts(eng, out, in0, s1, s2, op0, op1=None, extra_reads=()):
        if op1 is None:
            S.op(eng, lambda e, o=out, a=in0, x1=s1, p0=op0: e.tensor_scalar(o, a, x1, None, p0),
                 reads=[in0] + list(extra_reads), writes=[out])
        else:
            S.op(eng, lambda e, o=out, a=in0, x1=s1, x2=s2, p0=op0, p1=op1: e.tensor_scalar(o, a, x1, x2, p0, p1),
                 reads=[in0] + list(extra_reads), writes=[out])

    def stt(eng, out, in0, scalar, in1, op0, op1, extra_reads=()):
        S.op(eng, lambda e, o=out, a=in0, s=scalar, b=in1, p0=op0, p1=op1: e.scalar_tensor_tensor(o, a, s, b, p0, p1),
             reads=[in0, in1] + list(extra_reads), writes=[out])

    def cp(eng, out, in_):
        S.op(eng, lambda e, o=out, i=in_: e.tensor_copy(o, i), reads=[in_], writes=[out])

    def mset(eng, out, val):
        S.op(eng, lambda e, o=out, v=val: e.memset(o, v), writes=[out])

    def dma(q, out, in_, tag, reads=(), writes=()):
        S.op(q, lambda e, o=out, i=in_: e.dma_start(out=o, in_=i), reads=list(reads), writes=list(writes), dma=tag)

    def kp(ap2d, k):
        return ap2d.rearrange("(k p) n -> p k n", p=128)

    dma("pool", Wei[:], kp(d_wei, 8), "w0", writes=[Wei[:]])
    dma("pool", Weo[:], kp(d_weo, 8), "w1", writes=[Weo[:]])
    dma("pool", Woi[:], kp(d_woi, 8), "w2", writes=[Woi[:]])
    dma("pool", Wog[:], d_wog.rearrange("p (g k n) -> p g k n", g=4, k=2), "w3", writes=[Wog[:]])
    dma("sp", pcols[:], d_pcols, "c0", writes=[pcols[:]])
    dma("sp", cst[:], d_consts, "c1", writes=[cst[:]])
    dma("sp", tflat[:, 256:640], d_consts2, "c5", writes=[tflat[:, 256:640]])
    dma("sp", w2aug[:], d_w2aug, "c2", writes=[w2aug[:]])
    dma("sp", flags[:], d_flags, "c3", writes=[flags[:]])
    csT = tmp32[:, 0:1, 0:8 * R].rearrange("p a (k r) -> p (a k) r", k=8)
    dma("sp", tmp32[:, 0, 0:8 * R], d_cT, "c4", writes=[tmp32[:, 0, 0:8 * R]])
    act(tmp32[:, 0, 0:8 * R], tmp32[:, 0, 0:8 * R], AF.Silu)
    for hf in range(4):
        stg = A32[:, 0:2048].rearrange("p (k n) -> p k n", k=2)
        dma("sp", stg, kp(d_woo, 8)[:, hf * 2:(hf + 1) * 2, :], "w4", writes=[stg])
        for k4 in range(2):
            kc = hf * 2 + k4
            ts("dve", Woo[:, kc, :], stg[:, k4, :], pcols[:, PC_SCALE + kc:PC_SCALE + kc + 1], None, ALU.mult,
               extra_reads=[pcols[:]])
    mset("dve", ones1024[:], 1.0 / 1024)
    mset("dve", ones512[:], 1.0 / 512)
    mset("dve", ones128[:], 1.0 / 128)
    mset("pool", lrT[:], 1.0)
    mset("pool", SB32[:], 0.0)
    mset("pool", glu32[:], 0.0)
    mset("pool", vp32[:], 0.0)
    mset("pool", QZf, 0.0)
    for h in range(2):
        cp("dve", maskb[:, 0, h, :], maskP32)
        cp("dve", maskb[:, 1, h, :], maskS32)
    cp("dve", identb[:], ident32)
    NCH = 12
    stage = [x[:, :, :], vp32[:, :, 0:256]]
    pi = 0
    for li in range(2):
        for pc in range(NCH):
            st = stage[pi % 2]
            dma("sp", st, kp(d_ada[li], 8)[:, :, pc * 256:(pc + 1) * 256], f"a{pi % 2}", writes=[st])
            for sub in range(2):
                oc = pc * 2 + sub
                ps = PS()
                mmgroup(ps[:, 0:R], [(st[:, kc, sub * 128:(sub + 1) * 128], csT[:, kc, :]) for kc in range(8)],
                        reads=[st, tmp32[:, 0, 0:8 * R]])
                w, ko = oc // 8, oc % 8
                bcol = pcols[:, PC_ADAB + li * 24 + oc:PC_ADAB + li * 24 + oc + 1]
                if w == 1:
                    ts("dve", mod[:, li, 1, ko, :], ps[:, 0:R], bcol, 1.0, ALU.add, ALU.add, extra_reads=[pcols[:]])
                    ts("dve", mod[:, li, 1, ko, :], mod[:, li, 1, ko, :],
                       pcols[:, PC_NORMG + li * 8 + ko:PC_NORMG + li * 8 + ko + 1], None, ALU.mult, extra_reads=[pcols[:]])
                else:
                    ts("dve", mod[:, li, w, ko, :], ps[:, 0:R], bcol, None, ALU.add, extra_reads=[pcols[:]])
            pi += 1
    S.barrier()
    mset("pool", x[:], 0.0)
    mset("pool", x2[:], 0.0)
    mset("pool", vp32[:], 0.0)
    S.barrier()
    S.const = {t.tensor.name for t in (Wei[:], Weo[:], Woi[:], Wog[:], Woo[:], pcols[:], mod[:], ones1024[:], ones512[:],
                                       ones128[:], cst[:], maskb[:], w2aug[:], flags[:], identb[:])}

    def pc(i):
        return pcols[:, i:i + 1]

    def step_views(kind, kidx):
        if kind == "P":
            T, nseg, L = TP, 1, TP
            xsrc = kp(d_xT, 8)[:, :, kidx * TP:(kidx + 1) * TP]
        else:
            T, nseg, L = TS, 4, LS
            xsrc = kp(d_xsT, 8)[:, :, kidx * TS:(kidx + 1) * TS]
        G = glu32[:, :, 0:nseg * (30 + L)].rearrange("p c (s x) -> p c s x", s=nseg)
        V = vp32[:, :, 0:nseg * (15 + L)].rearrange("p c (s x) -> p c s x", s=nseg)
        return T, nseg, L, xsrc, G, V

    def step_rows(kind, kidx):
        return [0] if kind == "P" else [1 + 4 * kidx + i for i in range(4)]

    def stagings(T):
        h0 = hb[:, :, :].rearrange("p a b -> p (a b)").bitcast(F32)[:, 0:4 * T].rearrange("p (k t) -> p k t", k=4)
        h1 = A16[:, 5632:7680].bitcast(F32)[:, 0:4 * T].rearrange("p (k t) -> p k t", k=4)
        return h0, h1

    def prologue_early(kind, kidx, xbuf):
        T, nseg, L, xsrc, G, V = step_views(kind, kidx)
        dma("sp", xbuf[:, :, 0:T], xsrc, "xin", writes=[xbuf[:, :, 0:T]])
        if fused and kidx > 1:
            rbuf = (rcv if kind == "P" else rcvS)[(kidx - 2) % 3]
            rkey = ("rcv", kind, (kidx - 2) % 3)
            h0, h1 = stagings(T)
            rv = kp(rbuf[0:D, :], 8)
            dma("sp", h0, rv[:, 0:4, :], "rin0", reads=[rkey], writes=[h0])
            dma("sp", h1, rv[:, 4:8, :], "rin1", reads=[rkey], writes=[h1])
        if kind == "S":
            ci_ = d_convin.rearrange("p (c k s x) -> p c k s x", c=4, k=NSS, s=4)[:, :, kidx, :, :]
            for cc in range(4):
                dma("sp", G[:, cc, :, 0:30], ci_[:, cc, :, :], "sin0", writes=[G[:, cc, :, 0:30]])
            gi_ = d_glain.rearrange("p (k s j v) -> p k s j v", k=NSS, s=4, j=2)[:, kidx, :, :, :]
            dma("sp", SB32[:, 0:4, :, :], gi_, "sin1", writes=[SB32[:, 0:4, :, :]])

    def prologue_blend(kind, kidx, xbuf):
        T, nseg, L, xsrc, G, V = step_views(kind, kidx)
        if fused and kidx > 1:
            h0, h1 = stagings(T)
            stt("dve", xbuf[:, 0:4, 0:T], h0, flags[:, 0:1], xbuf[:, 0:4, 0:T], ALU.mult, ALU.add)
            stt("dve", xbuf[:, 4:8, 0:T], h1, flags[:, 0:1], xbuf[:, 4:8, 0:T], ALU.mult, ALU.add)

    def prologue_late(kind, kidx):
        T, nseg, L, xsrc, G, V = step_views(kind, kidx)
        if kind == "S":
            pi_ = d_poolin.rearrange("p (c k s x) -> p c k s x", c=8, k=NSS, s=4)[:, :, kidx, :, :]
            for cc in range(8):
                dma("sp", V[:, cc, :, 0:15], pi_[:, cc, :, :], "sin2", writes=[V[:, cc, :, 0:15]])
        if kind == "P" and fused and kidx == 2:
            ts("pool", SB32[:, 0, :, :], SB32[:, 0, :, :], flags[:, 1:2], None, ALU.mult)
            ts("pool", glu32[:, :, 0:30], glu32[:, :, 0:30], flags[:, 1:2], None, ALU.mult)
            ts("pool", vp32[:, :, 0:15], vp32[:, :, 0:15], flags[:, 1:2], None, ALU.mult)
        if kind == "S" and kidx == 0:
            mset("pool", QZf, 0.0)

    def modulate_gen(li, xbuf, T, nseg, L, rows, sq, rs, part="all"):
        if part in ("all", "stats"):
            mod_stats(xbuf, T, sq, rs)
        if part in ("all", "apply"):
            mod_apply(li, xbuf, T, nseg, L, rows, rs)

    def mod_stats(xbuf, T, sq, rs):
        for kc in range(8):
            act(sq[:, kc, 0:T], xbuf[:, kc, 0:T], AF.Square)
        ps = PS()
        mmgroup(ps[:, 0:T], [(ones1024[:], sq[:, kc, 0:T]) for kc in range(8)], reads=[sq[:, :, 0:T]])
        act(rs[:, 0:T], ps[:, 0:T], AF.Ln, bias=EPS, scale=1.0)
        act(rs[:, 0:T], rs[:, 0:T], AF.Exp, scale=-0.5)

    def mod_apply(li, xbuf, T, nseg, L, rows, rs):
        for kc in range(8):
            tb_ = tmp32[:, kc % 3, 0:T]
            for si in range(nseg):
                r = rows[si]
                sl = slice(si * L, (si + 1) * L)
                stt("dve", tb_[:, sl], xbuf[:, kc, sl], mod[:, li, 1, kc, r:r + 1], rs[:, sl], ALU.mult, ALU.mult)
                act(hb[:, kc, sl], tb_[:, sl], AF.Identity, bias=mod[:, li, 0, kc, r:r + 1], scale=1.0)

    class _Stop(Exception):
        pass
    _stage = int(os.environ.get("DBG_STAGE", "0"))

    def chk(n):
        if n == _stage:
            raise _Stop()

    def emit_step(kind, sidx, kidx, nxt, premod):
        xc = xb[sidx % 2]
        xn = xb[(sidx + 1) % 2]
        if kind == "P":
            T, nseg, L, C, nblk, cpb = TP, 1, TP, 64, 2, 2
            rows = [0]
            xsrc = kp(d_xT, 8)[:, :, kidx * TP:(kidx + 1) * TP]
            yoff = kidx * TP
            Ublk, SU, cmask, mk = UblkP, SUP, cmaskP, 0
        else:
            T, nseg, L, C, nblk, cpb = TS, 4, LS, 32, 1, 4
            rows = [1 + 4 * kidx + i for i in range(4)]
            xsrc = kp(d_xsT, 8)[:, :, kidx * TS:(kidx + 1) * TS]
            yoff = NPS * TP + kidx * TS
            Ublk, SU, cmask, mk = UblkS, SUS, cmaskS, 1
        G = glu32[:, :, 0:nseg * (30 + L)].rearrange("p c (s x) -> p c s x", s=nseg)
        V = vp32[:, :, 0:nseg * (15 + L)].rearrange("p c (s x) -> p c s x", s=nseg)
        QZ = QZf.rearrange("p (j t c n) -> p j t c n", j=2, t=nblk, c=cpb)
        KLZ = KLZf.rearrange("p (t c n) -> p t c n", t=nblk, c=cpb)

        def segv(ap):
            return ap.rearrange("p (s l) -> p s l", s=nseg)

        def modulate(li):
            modulate_gen(li, xc, T, nseg, L, rows, cat, rstd)

        sprinkle = [None]

        def proj_fm(W, col0, m=128):
            ps = PS()
            mmgroup(ps[0:m, 0:T], [(W[:, kc, col0:col0 + m], hb[:, kc, 0:T]) for kc in range(8)], reads=[hb[:, :, 0:T]])
            if sprinkle[0] is not None:
                sprinkle[0](5)
            return ps

        def residual(W, li, dst, ocs=range(8), ss=None):
            pend = None

            def flush(oc_prev):
                sqb, psb_ = ss
                mm(psb_[:, 0:T], ones1024[:], sqb[:, oc_prev, 0:T], oc_prev == 0, oc_prev == 7, [sqb[:, oc_prev, 0:T]], True)

            for oc in ocs:
                ps = PS()
                mmgroup(ps[:, 0:T], [(W[:, kc, oc * 128:(oc + 1) * 128], cat[:, kc, 0:T]) for kc in range(8)],
                        reads=[cat[:, :, 0:T]])
                if ss is not None and pend is not None:
                    flush(pend)
                for si in range(nseg):
                    r = rows[si]
                    sl = slice(si * L, (si + 1) * L)
                    stt("dve", dst[:, oc, sl], ps[:, sl], mod[:, li, 2, oc, r:r + 1], xc[:, oc, sl], ALU.mult, ALU.add)
                if ss is not None:
                    act(ss[0][:, oc, 0:T], dst[:, oc, 0:T], AF.Square)
                    pend = oc
            if ss is not None and pend is not None:
                flush(pend)

        chk(1)
        if not premod:
            modulate(0)
        chk(2)
        ps = proj_fm(Wei, 3072, m=16)
        act(lrT[0:16, 0:T], ps[0:16, 0:T], AF.Copy)
        tsls = [slice(tb * 128, (tb + 1) * 128) for tb in range(nblk)]
        pszs = []
        for tb in range(nblk):
            psz = PS()
            mm(psz[:, 0:256], lrT[0:17, tsls[tb]], w2aug[0:17, :], True, True, [lrT[0:17, tsls[tb]]], True)
            pszs.append(psz)
        for tb in range(nblk):
            act(loga[:, tb, :], pszs[tb][:, 0:256], AF.Exp, scale=-1.0)
        for tb in range(nblk):
            act(loga[:, tb, :], loga[:, tb, :], AF.Ln, bias=1.0, scale=1.0)
        for cc in range(4):
            psg = proj_fm(Wei, 512 + cc * 128)
            sig = tmp32[:, cc % 3, 0:T]
            act(sig, psg[:, 0:T], AF.Sigmoid)
            psv = proj_fm(Wei, cc * 128)
            tt("dve", G[:, cc, :, 30:30 + L], segv(psv[:, 0:T]), segv(sig), ALU.mult)
        conv_ops = []
        cpsum = [PSB[6], PSB[7]]
        pcount = [0]
        for ccp in range(2):
            for j in range(31):
                for cc in (2 * ccp, 2 * ccp + 1):
                    def one(cc=cc, j=j):
                        i_ = G[:, cc, :, j:j + L]
                        wc = pc(PC_CONVW + cc * 31 + j)
                        o_ = segv(c32[:, cc, 0:T])
                        if j not in DVE_TAPS:
                            slot = prodr[:, pcount[0] % 3, 0:T]
                            pcount[0] += 1
                            act(segv(slot), i_, AF.Copy, scale=wc)
                            mm(cpsum[cc % 2][:, 0:T], identb[:], slot, j == 0, j == 30, [slot], True)
                        elif j == 1:
                            ts("dve", o_, i_, wc, pc(PC_CONVB + cc), ALU.mult, ALU.add)
                        else:
                            stt("dve", o_, i_, wc, o_, ALU.mult, ALU.add)
                        if j == 30:
                            tt("dve", c32[:, cc, 0:T], c32[:, cc, 0:T], cpsum[cc % 2][:, 0:T], ALU.add)
                            act(cb[:, cc, 0:T], c32[:, cc, 0:T], AF.Copy)
                            act(c2b[:, cc, 0:T], c32[:, cc, 0:T], AF.Square)
                    conv_ops.append(one)
        conv_pos = [0]

        def pump(n):
            for _ in range(n):
                if conv_pos[0] < len(conv_ops):
                    conv_ops[conv_pos[0]]()
                    conv_pos[0] += 1

        chk(3)
        sprinkle[0] = pump
        pump(10)
        chk(4)
        pump(12)
        psbs, psds = [], []
        for tb in range(nblk):
            psb = PS()
            for j in range(2):
                mm(psb[:, j * 128:(j + 1) * 128], loga[:, tb, j * 128:(j + 1) * 128], Ublk, True, True,
                   [loga[:, tb, :]], True)
            psd = PS()
            mm(psd[:, 0:256], SU, loga[:, tb, :], True, True, [loga[:, tb, :]], True)
            psbs.append(psb)
            psds.append(psd)
        for tb in range(nblk):
            pb2 = psbs[tb][:, 0:256].rearrange("p (j n) -> p j n", j=2)
            act(expb[:, :, tsls[tb]], pb2, AF.Exp)
            act(expnb[:, :, tsls[tb]], pb2, AF.Exp, scale=-1.0)
            act(expd[:, tb, :], psds[tb][:, 0:256], AF.Exp)
        pump(12)
        chk(5)
        for j in range(2):
            ps = proj_fm(Wei, 1536 + j * 128)
            stt("dve", QE[:, j, 0:T], ps[:, 0:T], 0.125, expb[:, j, 0:T], ALU.mult, ALU.mult)
            for c in range(4):
                tb, ci = c // cpb, c % cpb
                cp("dve", QZ[:, j, tb, ci, ci * C:(ci + 1) * C], QE[:, j, c * C:(c + 1) * C])
            ps = proj_fm(Wei, 1792 + j * 128)
            tt("dve", KE[:, j, 0:T], ps[:, 0:T], expnb[:, j, 0:T], ALU.mult)
            pump(12)
        for tb in range(nblk):
            tsl = slice(tb * 128, (tb + 1) * 128)
            ps = PS()
            mmgroup(ps[:, 0:512], [(hb[:, kc, tsl], Wei[:, kc, 2048:2560]) for kc in range(8)], reads=[hb[:, :, tsl]])
            act(v_tm[:, tb, :], ps[:, 0:512], AF.Copy)
            ps = PS()
            mmgroup(ps[:, 0:256], [(hb[:, kc, tsl], Wei[:, kc, 1792:2048]) for kc in range(8)], reads=[hb[:, :, tsl]])
            for ci in range(cpb):
                stt("dve", KLZ[:, tb, ci, :], ps[:, 0:256], cmask[:, ci:ci + 1], expd[:, tb, :], ALU.mult, ALU.mult)
            pump(12)
        for cc in range(4):
            ps = proj_fm(Wei, 1024 + cc * 128)
            act(sga[:, cc, 0:T], ps[:, 0:T], AF.Silu)
        for cc in range(4):
            ps = proj_fm(Wei, 2560 + cc * 128)
            act(sgb[:, cc, 0:T], ps[:, 0:T], AF.Silu)
        chk(6)
        chk(7)
        cp("dve", Sb16[:, 0, :, :], SB32[:, 0, :, :]) if kind == "P" else cp("dve", Sb16[:, :, :, :], SB32[:, 0:4, :, :])
        for c in range(4):
            tb, ci = c // cpb, c % cpb
            psu = PS()
            for j in range(2):
                mm(psu[:, j * 256:(j + 1) * 256], KLZ[:, tb, ci, j * 128:(j + 1) * 128], v_tm[:, tb, j * 256:(j + 1) * 256],
                   True, True, [KLZ[:, tb, ci, :], v_tm[:, tb, :]], True)
            for j in range(2):
                for hh in range(2):
                    prt = slice(64 * hh, 64 * hh + 64)
                    dec = expb[prt, j, c * C + C - 1:c * C + C]
                    if kind == "P":
                        src, dst = SB32[prt, c, j, :], SB32[prt, c + 1, j, :]
                    else:
                        src, dst = SB32[prt, c, j, :], SB32[prt, c, j, :]
                    stt("dve", dst, src, dec, psu[prt, j * 256 + hh * 128:j * 256 + hh * 128 + 128], ALU.mult, ALU.add,
                        extra_reads=[dec])
            if kind == "P" and c < 3:
                cp("dve", Sb16[:, c + 1, :, :], SB32[:, c + 1, :, :])
            pump(8)
        chk(8)
        attm_l = [attm, carve(A16, 9728, [4, 128])]
        osq_l = [osq, carve(A16, 5632, [512])]
        rstdo_l = [rstdo, carve(A32, 2560, [512])]
        pso_l, psq_l = [], []
        for tb in range(nblk):
            tsl = slice(tb * 128, (tb + 1) * 128)
            am = attm_l[tb]
            psa2 = [PS(), PS()]
            for hh in range(2):
                prt = slice(64 * hh, 64 * hh + 64)
                for j in range(2):
                    mm(psa2[hh][:, j * 128:(j + 1) * 128], KE[prt, j, tsl], QE[prt, j, tsl], True, True,
                       [KE[prt, j, tsl], QE[prt, j, tsl]], True)
            attv = am.rearrange("p (j hh) n -> p j hh n", hh=2)
            for hh in range(2):
                tt("dve", attv[:, :, hh, :], psa2[hh][:, 0:256].rearrange("p (j n) -> p j n", j=2), maskb[:, mk, 0:2, :], ALU.mult)
            pso = PS()
            for h in range(4):
                j, hh = h // 2, h % 2
                prt = slice(64 * hh, 64 * hh + 64)
                oh = pso[:, h * 128:(h + 1) * 128]
                mm(oh, v_tm[:, tb, h * 128:(h + 1) * 128], am[:, h, :], True, False, [v_tm[:, tb, :], am[:, h, :]], False)
                for ci in range(cpb):
                    c = tb * cpb + ci
                    mm(oh, Sb16[prt, c, j, :], QZ[prt, j, tb, ci, :], False, ci == cpb - 1,
                       [Sb16[prt, c, j, :], QZ[prt, j, tb, ci, :]], ci == cpb - 1)
            act(osq_l[tb][:], pso[:, 0:512], AF.Square)
            psq = PS()
            mm(psq[:, 0:512], ones128[:], osq_l[tb][:], True, True, [osq_l[tb][:]], True)
            pso_l.append(pso)
            psq_l.append(psq)
            pump(10)
        pump(1000)
        psm, pse = PSB[6], PSB[7]
        mmgroup(psm[:, 0:T], [(ones512[:], cb[:, cc, 0:T]) for cc in range(4)], reads=[cb[:, :, 0:T]])
        mmgroup(pse[:, 0:T], [(ones512[:], c2b[:, cc, 0:T]) for cc in range(4)], reads=[c2b[:, :, 0:T]])
        act(meanb[:, 0:T], psm[:, 0:T], AF.Copy)
        act(varb[:, 0:T], psm[:, 0:T], AF.Square)
        tt("dve", varb[:, 0:T], pse[:, 0:T], varb[:, 0:T], ALU.subtract)
        for tb in range(nblk):
            act(rstdo_l[tb][:], psq_l[tb][:, 0:512], AF.Ln, bias=EPS, scale=1.0)
            act(rstdo_l[tb][:], rstdo_l[tb][:], AF.Exp, scale=-0.5)
        act(varb[:, 0:T], varb[:, 0:T], AF.Ln, bias=EPS, scale=1.0)
        act(varb[:, 0:T], varb[:, 0:T], AF.Exp, scale=-0.5)
        for tb in range(nblk):
            tsl = slice(tb * 128, (tb + 1) * 128)
            ro = rstdo_l[tb]
            tt("dve", ro[:], pso_l[tb][:, 0:512], ro[:], ALU.mult)
            for h in range(4):
                stt("dve", cat[:, 4 + h, tsl], ro[:, h * 128:(h + 1) * 128],
                    pc(PC_HEADG + h), sgb[:, h, tsl], ALU.mult, ALU.mult)
        for cc in range(4):
            t1 = tmp32[:, cc % 3, 0:T]
            tt("dve", t1, c32[:, cc, 0:T], meanb[:, 0:T], ALU.subtract)
            tt("dve", t1, t1, varb[:, 0:T], ALU.mult)
            act(ub[:, cc % 2, 0:T], t1, AF.Silu, bias=pc(PC_LNB + cc), scale=pc(PC_LNG + cc))
            tt("dve", cat[:, cc, 0:T], ub[:, cc % 2, 0:T], sga[:, cc, 0:T], ALU.mult)
        chk(9)
        pump(1000)
        chk(10)
        if kind == "P":
            cp("pool", SB32[:, 0, :, :], SB32[:, 4, :, :])
        slot = None
        if kind == "P":
            if kidx == NPS - 3:
                slot = 0
            elif kidx == NPS - 1:
                slot = 1
        co = d_convout.rearrange("p (c s x) -> p c s x", c=4, s=NSLOT)
        go = d_glaout.rearrange("p (s j v) -> p s j v", s=NSLOT, j=2)
        po = d_poolout.rearrange("p (c s x) -> p c s x", c=8, s=NSLOT)
        if kind == "P":
            if slot is not None:
                dma("sp", co[:, :, slot, :], G[:, :, 0, L:L + 30], "so0", reads=[G[:, :, 0, L:L + 30]])
                dma("sp", go[:, slot, :, :], SB32[:, 0, :, :], "so1", reads=[SB32[:, 0, :, :]])
            cp("pool", G[:, :, 0, 0:30], G[:, :, 0, L:L + 30])
        else:
            s0 = 2 + 4 * kidx
            for cc in range(4):
                dma("sp", co[:, cc, s0:s0 + 4, :], G[:, cc, :, L:L + 30], "so0", reads=[G[:, cc, :, L:L + 30]])
            dma("sp", go[:, s0:s0 + 4, :, :], SB32[:, 0:4, :, :], "so1", reads=[SB32[:, 0:4, :, :]])
        chk(11)
        residual(Weo, 0, xc, ss=(sqn, PSB[6]))
        chk(12)

        act(rstd[:, 0:T], PSB[6][:, 0:T], AF.Ln, bias=EPS, scale=1.0)
        act(rstd[:, 0:T], rstd[:, 0:T], AF.Exp, scale=-0.5)
        modulate_gen(1, xc, T, nseg, L, rows, None, rstd, part="apply")
        if kind == "P":
            stt("dve", icb[:, :].rearrange("p g n -> p (g n)"), delta, flags[:, 2 + kidx:3 + kidx], invw, ALU.mult, ALU.add)
        for oc in range(8):
            ps = proj_fm(Woi, oc * 128)
            act(V[:, oc, :, 15:15 + L], segv(ps[:, 0:T]), AF.Copy)
        for oc in range(8):
            ps = proj_fm(Woi, 1024 + oc * 128)
            act(sg[:, oc, 0:T], ps[:, 0:T], AF.Silu)
        if nxt is not None:
            prologue_early(*nxt, xn)
        XL = 15 + L
        for gi in range(4):
            w = float(1 << (gi + 1))
            for c2 in range(2):
                chn = 2 * gi + c2
                src = V[:, chn, :, :]
                lo = 0
                for lev in range(gi + 1):
                    sh = 1 << lev
                    dstb = win[lev % 2][:, c2, 0:nseg * XL].rearrange("p (s x) -> p s x", s=nseg)
                    nlo = lo + sh
                    tt("dve", dstb[:, :, nlo:XL], src[:, :, nlo:XL], src[:, :, nlo - sh:XL - sh], ALU.add)
                    src = dstb
                    lo = nlo
                pbv = pb[:, chn, 0:T].rearrange("p (s l) -> p s l", s=nseg)
                if kind == "P":
                    stt("dve", pbv[:, :, 16:L], src[:, :, 31:XL], 1.0 / w, V[:, chn, :, 31:XL], ALU.mult, ALU.subtract)
                    t16 = tmp32[:, c2, 0:16]
                    tt("dve", t16, src[:, 0, 15:31], icb[:, gi, :], ALU.mult)
                    tt("dve", pb[:, chn, 0:16], t16, V[:, chn, 0, 15:31], ALU.subtract)
                else:
                    stt("dve", pbv, src[:, :, 15:XL], 1.0 / w, V[:, chn, :, 15:XL], ALU.mult, ALU.subtract)
        if nxt is not None:
            prologue_blend(*nxt, xn)
            nk_, ni_ = nxt
            Tn, nsegn, Ln, _, _, _ = step_views(nk_, ni_)
            modulate_gen(0, xn, Tn, nsegn, Ln, step_rows(nk_, ni_), sqn, rstd, part="stats")
        for gi in range(4):
            for o2 in range(2):
                ps = PS()
                mmgroup(ps[:, 0:T], [(Wog[:, gi, k2, o2 * 128:(o2 + 1) * 128], pb[:, 2 * gi + k2, 0:T]) for k2 in range(2)],
                        reads=[pb[:, 2 * gi:2 * gi + 2, 0:T]])
                oc = 2 * gi + o2
                stt("dve", cat[:, oc, 0:T], ps[:, 0:T], pc(PC_GRPB + oc), sg[:, oc, 0:T], ALU.add, ALU.mult)
        if kind == "P":
            if slot is not None:
                dma("sp", po[:, :, slot, :], V[:, :, 0, L:L + 15], "so2", reads=[V[:, :, 0, L:L + 15]])
            cp("pool", V[:, :, 0, 0:15], V[:, :, 0, L:L + 15])
        else:
            s0 = 2 + 4 * kidx
            for cc in range(8):
                dma("sp", po[:, cc, s0:s0 + 4, :], V[:, cc, :, L:L + 15], "so2", reads=[V[:, cc, :, L:L + 15]])
        chk(13)
        if nxt is not None:
            modulate_gen(0, xn, Tn, nsegn, Ln, step_rows(nk_, ni_), sqn, rstd, part="apply")
        residual(Woo, 1, xc)
        chk(14)

        nk = NPS if kind == "P" else NSS
        if fused and kidx < nk - 2:
            par = kidx % 3
            sbuf_ = (snd if kind == "P" else sndS)[par]
            rbuf_ = (rcv if kind == "P" else rcvS)[par]
            skey = ("snd", kind, par)
            rkey2 = ("rcv", kind, par)
            dma("pool", kp(sbuf_, 8), xc[:, :, 0:T], "xs", reads=[xc[:, :, 0:T], skey], writes=[skey])
            S.op("pool", lambda e, a=sbuf_, b=rbuf_: e.collective_compute(
                "AllGather", ALU.bypass, replica_groups=[[0, 1], [2, 3], [4, 5], [6, 7]], ins=[a], outs=[b]),
                reads=[skey], writes=[rkey2], dma="cc")
        if not fused:
            dma("sp", kp(d_xoT, 8)[:, :, yoff:yoff + T], xc[:, :, 0:T], "xout", reads=[xc[:, :, 0:T]])
        sq = sg
        for kc in range(8):
            act(sq[:, kc, 0:T], xc[:, kc, 0:T], AF.Square)
        ps = PS()
        mmgroup(ps[:, 0:T], [(ones1024[:], sq[:, kc, 0:T]) for kc in range(8)], reads=[sq[:, :, 0:T]])
        act(rstd_fn[:, 0:T], ps[:, 0:T], AF.Ln, bias=EPS, scale=1.0)
        act(rstd_fn[:, 0:T], rstd_fn[:, 0:T], AF.Exp, scale=-0.5)
        for kc in range(8):
            stt("dve", xc[:, kc, 0:T], xc[:, kc, 0:T], pc(PC_FING + kc), rstd_fn[:, 0:T], ALU.mult, ALU.mult)
        dma("sp", kp(d_yT, 8)[:, :, yoff:yoff + T], xc[:, :, 0:T], "yout", reads=[xc[:, :, 0:T]])
        if nxt is not None:
            prologue_late(*nxt)

    _dbg = int(os.environ.get("DBG_STEPS", "999"))
    steps = [("P", k) for k in range(NPS)] + [("S", k) for k in range(NSS)]
    steps = steps[:_dbg]
    if steps:
        prologue_early(*steps[0], xb[0])
        prologue_blend(*steps[0], xb[0])
        prologue_late(*steps[0])
    for sidx, (kd, k) in enumerate(steps):
        nxt = steps[sidx + 1] if sidx + 1 < len(steps) else None
        try:
            emit_step(kd, sidx, k, nxt, sidx > 0)
        except _Stop:
            pass
    S.barrier()

    sems = {}
    for sn in sorted(S.semnames):
        sems[sn] = es.enter_context(nc.semaphore(sn))
    S.emit(nc, sems)
    es.close()
    return nc


def _fm(a2d):
    r, f = a2d.shape
    return np.ascontiguousarray(a2d.T.reshape(f // 128, 128, r).transpose(1, 0, 2))


def _consts():
    c = np.zeros((128, 1032), np.float32)
    s = np.arange(128)[:, None]
    t = np.arange(128)[None, :]
    for i, C in enumerate((64, 32)):
        same = (s // C) == (t // C)
        U = (same & (s <= t)).astype(np.float32)
        SU = (same & (s > t)).astype(np.float32)
        c[:, i * 128:(i + 1) * 128] = -U / 16.0
        c[:, 256 + i * 128:256 + (i + 1) * 128] = -SU / 16.0
        c[:, 512 + i * 128:512 + (i + 1) * 128] = U
    c[:, 904:1032] = np.eye(128, dtype=np.float32)
    p = np.arange(128)
    for ci in range(2):
        c[:, 768 + ci] = (p // 64 == ci)
    for ci in range(4):
        c[:, 772 + ci] = (p // 32 == ci)
    for gi in range(4):
        w = 2 ** (gi + 1)
        tt_ = np.arange(16)
        c[:, 776 + gi * 16:776 + (gi + 1) * 16] = (1.0 / np.minimum(tt_ + 1, w) - 1.0 / w)[None, :]
        c[:, 840 + gi * 16:840 + (gi + 1) * 16] = 1.0 / w
    c1 = np.ascontiguousarray(np.concatenate([c[:, 0:512], c[:, 768:904]], axis=1))
    c2 = np.ascontiguousarray(np.concatenate([c[:, 512:768], c[:, 904:1032]], axis=1))
    return c1, c2


def _pack_core(inp, role, q, NPS, NSS, fused, seqlen, x_override=None, xs_override=None):
    le = 0 if role == 0 else 2
    e = le // 2
    lo_ = le + 1
    o = lo_ // 2
    R = 1 + 4 * NSS
    f32 = np.float32
    m = {}
    shift = 2 if (fused and role == 1) else 0
    xT = np.zeros((D, NPS * TP), f32)
    if x_override is not None:
        xT[:, :x_override.shape[1]] = x_override
    elif not (fused and role == 1):
        xT[:, :seqlen] = inp["x_prompt"][q, :seqlen].T
    m["xT"] = xT
    xsT = np.zeros((D, NSS * TS), f32)
    cT = np.zeros((R, D), f32)
    cT[0] = inp["c_prompt"][q]
    conv_in = np.zeros((128, 4, NSS, 4, 30), f32)
    gla_in = np.zeros((128, NSS, 4, 2, 128), f32)
    pool_in = np.zeros((128, 8, NSS, 4, 15), f32)
    for k in range(2):
        ks = k + shift
        if ks >= NSS:
            continue
        for i in range(4):
            sq_ = 8 * q + 4 * k + i
            if xs_override is not None:
                xsT[:, ks * TS + i * LS: ks * TS + (i + 1) * LS] = xs_override[:, k * TS + i * LS:k * TS + (i + 1) * LS]
            elif not (fused and role == 1):
                xsT[:, ks * TS + i * LS: ks * TS + (i + 1) * LS] = inp["x_sample"][sq_].T
            cT[1 + 4 * ks + i] = inp["c_sample"][sq_]
            conv_in[:, :, ks, i, :] = inp["state_conv"][e, sq_].T.reshape(4, 128, 30).transpose(1, 0, 2)
            gla_in[:, ks, i, :, :] = inp["state_gla"][e, sq_].reshape(2, 128, 128).transpose(1, 0, 2)
            pool_in[:, :, ks, i, :] = inp["state_pool"][o, sq_].T.reshape(8, 128, 15).transpose(1, 0, 2)
    m["xsT"] = xsT
    m["cT"] = np.ascontiguousarray(cT.T.reshape(8, 128, R).transpose(1, 0, 2)).reshape(128, 8 * R)
    m["conv_in"] = conv_in.reshape(128, -1)
    m["gla_in"] = gla_in.reshape(128, -1)
    m["pool_in"] = pool_in.reshape(128, -1)
    m["w_ev_in"] = np.ascontiguousarray(inp["ev_w_in"][e])
    m["w_ev_out"] = np.ascontiguousarray(inp["ev_w_out"][e])
    m["w_od_in"] = np.ascontiguousarray(inp["od_w_in"][o])
    gw = inp["od_group_w"][o]
    m["w_od_grp"] = np.ascontiguousarray(gw.reshape(4, 2, 128, 256).transpose(2, 0, 1, 3)).reshape(128, -1)
    m["w_od_out"] = np.ascontiguousarray(inp["od_w_out"][o])
    m["ada_w"] = np.ascontiguousarray(np.stack([inp["ada_w"][le], inp["ada_w"][lo_]]))
    pcl = np.zeros((128, NPC), f32)

    def col(v):
        return v.reshape(-1, 128).T
    pcl[:, PC_NORMG:PC_NORMG + 8] = col(inp["norm_g"][le])
    pcl[:, PC_NORMG + 8:PC_NORMG + 16] = col(inp["norm_g"][lo_])
    pcl[:, PC_CONVB:PC_CONVB + 4] = col(inp["ev_conv_b"][e])
    pcl[:, PC_LNG:PC_LNG + 4] = col(inp["ev_ln_g"][e])
    pcl[:, PC_LNB:PC_LNB + 4] = col(inp["ev_ln_b"][e])
    pcl[:, PC_HEADG:PC_HEADG + 4] = col(inp["ev_head_g"][e])
    pcl[:, PC_GRPB:PC_GRPB + 8] = col(inp["od_group_b"][o])
    pcl[:, PC_SCALE:PC_SCALE + 8] = col(inp["od_scale"][o])
    pcl[:, PC_FING:PC_FING + 8] = col(inp["final_g"])
    cw = inp["ev_conv_w"][e]
    pcl[:, PC_CONVW:PC_CONVW + 124] = cw.T.reshape(4, 128, 31).transpose(1, 0, 2).reshape(128, 124)
    pcl[:, PC_ADAB:PC_ADAB + 24] = col(inp["ada_b"][le])
    pcl[:, PC_ADAB + 24:PC_ADAB + 48] = col(inp["ada_b"][lo_])
    m["pcols"] = pcl
    w2 = np.zeros((32, 256), f32)
    w2[0:16] = inp["ev_gate_w2"][e]
    w2[16] = inp["ev_gate_b"][e]
    m["w2aug"] = w2
    fl = np.zeros((128, 2 + NPS), f32)
    fl[:, 0] = 1.0 if role == 1 else 0.0
    fl[:, 1] = 0.0 if (fused and role == 1) else 1.0
    fl[:, 2 + shift] = 1.0
    m["flags"] = fl
    m["consts"], m["consts2"] = _consts()
    return m


def _unfm(a, rows):
    k = a.shape[1]
    return np.ascontiguousarray(a.transpose(2, 1, 0).reshape(rows, k * 128))


def run_all(inp, seqlen=8192, fused=True):
    NP = seqlen // TP
    outs = [np.zeros((4, seqlen, D), np.float32), np.zeros((32, 32, D), np.float32),
            np.zeros((2, 4, 30, 512), np.float32), np.zeros((2, 4, 4, 64, 128), np.float32),
            np.zeros((2, 4, 15, 1024), np.float32), np.zeros((2, 32, 30, 512), np.float32),
            np.zeros((2, 32, 4, 64, 128), np.float32), np.zeros((2, 32, 15, 1024), np.float32)]

    def collect(res, role, q, NPS, NSS, shift, final):
        NSLOT = 2 + 4 * NSS
        e = role
        co = res["conv_out"].reshape(128, 4, NSLOT, 30)
        go = res["gla_out"].reshape(128, NSLOT, 2, 128)
        po = res["pool_out"].reshape(128, 8, NSLOT, 15)
        pslot = 1 if (shift > 0 or not fused) else 0
        outs[2][e, q] = _unfm(co[:, :, pslot, :], 30)
        outs[3][e, q] = go[:, pslot].transpose(1, 0, 2).reshape(4, 64, 128)
        outs[4][e, q] = _unfm(po[:, :, pslot, :], 15)
        for k in range(2):
            ks = k + shift
            for i in range(4):
                sq_ = 8 * q + 4 * k + i
                sl = 2 + 4 * ks + i
                outs[5][e, sq_] = _unfm(co[:, :, sl, :], 30)
                outs[6][e, sq_] = go[:, sl].transpose(1, 0, 2).reshape(4, 64, 128)
                outs[7][e, sq_] = _unfm(po[:, :, sl, :], 15)
        if final:
            yT = res["yT"]
            outs[0][q] = yT[:, shift * TP: shift * TP + seqlen].T
            for k in range(2):
                ks = k + shift
                for i in range(4):
                    base = NPS * TP + ks * TS + i * LS
                    outs[1][8 * q + 4 * k + i] = yT[:, base:base + LS].T

    if fused:
        NPS, NSS = NP + 2, 4
        nc = build_program(NPS, NSS, True)
        maps = []
        for c in range(8):
            maps.append(_pack_core(inp, c % 2, c // 2, NPS, NSS, True, seqlen))
        res = run_bass_kernel_spmd(nc, maps, core_ids=list(range(8)))
        for c in range(8):
            collect(res.results[c], c % 2, c // 2, NPS, NSS, 2 * (c % 2), c % 2 == 1)
    else:
        NPS, NSS = NP, 2
        nc = build_program(NPS, NSS, False)
        maps = [_pack_core(inp, 0, q, NPS, NSS, False, seqlen) for q in range(4)]
        maps += [_pack_core(inp, 0, q, NPS, NSS, False, seqlen) for q in range(4)]
        res = run_bass_kernel_spmd(nc, maps, core_ids=list(range(8)))
        for q in range(4):
            collect(res.results[q], 0, q, NPS, NSS, 0, False)
        maps2 = []
        for q in range(4):
            yT = res.results[q]["xoT"]
            maps2.append(_pack_core(inp, 1, q, NPS, NSS, False, seqlen, x_override=yT[:, :seqlen],
                                    xs_override=yT[:, NPS * TP:]))
        maps2 += maps2
        nc2 = build_program(NPS, NSS, False)
        res2 = run_bass_kernel_spmd(nc2, maps2, core_ids=list(range(8)))
        for q in range(4):
            collect(res2.results[q], 1, q, NPS, NSS, 0, True)
    return tuple(outs)


def kernel(**inputs):
    inp = {k: np.asarray(v) for k, v in inputs.items()}
    return run_all(inp, seqlen=inp["x_prompt"].shape[1], fused=True)
```
